# Optimizing a Trainium2 kernel written in Bass

```python
import math
import jax, jax.numpy as jnp
from jax import lax
import numpy as np

D_MODEL = 1024
BATCH = 4
SEQ = 4096
DEPTH = 1

N_DIFF_HEADS = 8
DIFF_HEAD_DIM = 64
DIFF_V_DIM = 2 * DIFF_HEAD_DIM
DIFF_QK_WIDTH = N_DIFF_HEADS * 2 * DIFF_HEAD_DIM
DIFF_WIDTH = N_DIFF_HEADS * DIFF_V_DIM
Q_BLOCK = 128
CHUNK = 128
N_SG_GROUPS = 8
SG_WIDTH = D_MODEL
SG_GROUP_DIM = SG_WIDTH // N_SG_GROUPS
D_FF = 2816
CONV_WIDTH = 3
EPS = 1e-6
P_WIDTH = 2 * DIFF_QK_WIDTH + DIFF_WIDTH + 2 * SG_WIDTH + 2 * D_MODEL

kernel_name = "hybrid_diffattn_sgu_convffn_block"


def rmsnorm(x, g):
    xf = x.astype(jnp.float32)
    y = xf * lax.rsqrt(jnp.mean(xf * xf, axis=-1, keepdims=True) + EPS)
    return (y * g.astype(jnp.float32)).astype(x.dtype)


def lambda_init_for(layer):
    return 0.8 - 0.6 * math.exp(-0.3 * layer)


def diff_attention(q, k, v, lam):
    B, H, _, S, Dh = q.shape
    n_blocks = S // Q_BLOCK
    slopes = 2.0 ** (-8.0 * jnp.arange(1, H + 1, dtype=jnp.float32) / H)
    key_pos = jnp.arange(S, dtype=jnp.float32)

    def one_block(i):
        start = i * Q_BLOCK
        qb = lax.dynamic_slice_in_dim(q, start, Q_BLOCK, axis=3)
        s = jnp.einsum('bhmqd,bhmkd->bhmqk', qb, k).astype(jnp.float32)
        q_pos = (start + jnp.arange(Q_BLOCK)).astype(jnp.float32)
        dist = jnp.abs(q_pos[:, None] - key_pos[None, :])
        s = s - slopes[None, :, None, None, None] * dist[None, None, None]
        p = jax.nn.softmax(s, axis=-1)
        attn = p[:, :, 0] - lam * p[:, :, 1]
        return jnp.einsum('bhqk,bhkd->bhqd', attn.astype(v.dtype), v)

    o = lax.map(one_block, jnp.arange(n_blocks))
    return o.transpose(1, 0, 3, 2, 4).reshape(B, S, H, DIFF_V_DIM)


def depthwise_conv_seq(u, w, b):
    C = u.shape[-1]
    pad = CONV_WIDTH // 2
    y = lax.conv_general_dilated(u, w[:, None, :].astype(u.dtype), window_strides=(1,),
                                 padding=((pad, pad),),
                                 dimension_numbers=('NWC', 'WIO', 'NWC'),
                                 feature_group_count=C)
    return y + b


def setup_inputs(seed: int = 0) -> dict:
    key = jax.random.key(seed)
    ks = jax.random.split(key, 24)
    f32 = jnp.float32
    L = DEPTH

    def nrm(k, shape, scale):
        return jax.random.normal(k, shape, f32) * scale

    return {
        "x": jax.random.normal(ks[0], (BATCH, SEQ, D_MODEL), f32),
        "ln_mix_g": 1.0 + nrm(ks[1], (L, D_MODEL), 0.02),
        "w_in": nrm(ks[2], (L, D_MODEL, P_WIDTH), D_MODEL ** -0.5),
        "q_norm_g": 1.0 + nrm(ks[3], (L, DIFF_HEAD_DIM), 0.02),
        "k_norm_g": 1.0 + nrm(ks[4], (L, DIFF_HEAD_DIM), 0.02),
        "lambda_q1": nrm(ks[5], (L, DIFF_HEAD_DIM), 0.1),
        "lambda_k1": nrm(ks[6], (L, DIFF_HEAD_DIM), 0.1),
        "lambda_q2": nrm(ks[7], (L, DIFF_HEAD_DIM), 0.1),
        "lambda_k2": nrm(ks[8], (L, DIFF_HEAD_DIM), 0.1),
        "subln_g": 1.0 + nrm(ks[9], (L, DIFF_V_DIM), 0.02),
        "sg_norm_g": 1.0 + nrm(ks[10], (L, SG_WIDTH), 0.02),
        "sg_w": nrm(ks[11], (L, N_SG_GROUPS, CHUNK, CHUNK), CHUNK ** -0.5),
        "sg_b": 1.0 + nrm(ks[12], (L, N_SG_GROUPS, CHUNK), 0.1),
        "w_branch_a": nrm(ks[13], (L, DIFF_WIDTH, D_MODEL), DIFF_WIDTH ** -0.5),
        "w_branch_b": nrm(ks[14], (L, SG_WIDTH, D_MODEL), SG_WIDTH ** -0.5),
        "w_out": nrm(ks[15], (L, D_MODEL, D_MODEL), D_MODEL ** -0.5),
        "ln_ffn_g": 1.0 + nrm(ks[16], (L, D_MODEL), 0.02),
        "w_up": nrm(ks[17], (L, D_MODEL, 2 * D_FF), D_MODEL ** -0.5),
        "conv_w": nrm(ks[18], (L, CONV_WIDTH, 2 * D_FF), CONV_WIDTH ** -0.5),
        "conv_b": nrm(ks[19], (L, 2 * D_FF), 0.02),
        "w_down": nrm(ks[20], (L, D_FF, D_MODEL), D_FF ** -0.5),
    }


def reference(x, ln_mix_g, w_in, q_norm_g, k_norm_g, lambda_q1, lambda_k1, lambda_q2,
              lambda_k2, subln_g, sg_norm_g, sg_w, sg_b, w_branch_a, w_branch_b, w_out,
              ln_ffn_g, w_up, conv_w, conv_b, w_down):
    B, S, D = x.shape
    H, Dh = N_DIFF_HEADS, DIFF_HEAD_DIM
    G, Cg = N_SG_GROUPS, SG_GROUP_DIM
    n_chunks = S // CHUNK
    o_q, o_k = 0, DIFF_QK_WIDTH
    o_v = 2 * DIFF_QK_WIDTH
    o_u = o_v + DIFF_WIDTH
    o_sv = o_u + SG_WIDTH
    o_ga = o_sv + SG_WIDTH
    o_gb = o_ga + D_MODEL

    for l in range(DEPTH):
        lam_init = lambda_init_for(l)
        h = rmsnorm(x, ln_mix_g[l])
        p = h @ w_in[l]

        q = rmsnorm(p[..., o_q:o_k].reshape(B, S, H, 2, Dh), q_norm_g[l]) * (Dh ** -0.5)
        k = rmsnorm(p[..., o_k:o_v].reshape(B, S, H, 2, Dh), k_norm_g[l])
        v = p[..., o_v:o_u].reshape(B, S, H, DIFF_V_DIM)
        q = q.transpose(0, 2, 3, 1, 4)
        k = k.transpose(0, 2, 3, 1, 4)
        v = v.transpose(0, 2, 1, 3)
        lam = (jnp.exp(jnp.sum(lambda_q1[l].astype(jnp.float32) * lambda_k1[l].astype(jnp.float32)))
               - jnp.exp(jnp.sum(lambda_q2[l].astype(jnp.float32) * lambda_k2[l].astype(jnp.float32)))
               + lam_init)
        o_a = diff_attention(q, k, v, lam)
        o_a = (rmsnorm(o_a, subln_g[l]) * (1.0 - lam_init)).reshape(B, S, DIFF_WIDTH)

        u = jax.nn.gelu(p[..., o_u:o_sv])
        sv = rmsnorm(jax.nn.gelu(p[..., o_sv:o_ga]).reshape(B, S, G, Cg),
                     sg_norm_g[l].reshape(G, Cg))
        sv = sv.reshape(B, n_chunks, CHUNK, G, Cg)
        mixed = jnp.einsum('gts,bcsgd->bctgd', sg_w[l], sv) + sg_b[l].T[:, :, None]
        o_b = u * mixed.reshape(B, S, SG_WIDTH)

        merged = (jax.nn.sigmoid(p[..., o_ga:o_gb]) * (o_a @ w_branch_a[l])
                  + jax.nn.sigmoid(p[..., o_gb:]) * (o_b @ w_branch_b[l]))
        x = x + merged @ w_out[l]

        h2 = rmsnorm(x, ln_ffn_g[l])
        up = depthwise_conv_seq(h2 @ w_up[l], conv_w[l], conv_b[l])
        gate, val = up[..., :D_FF], up[..., D_FF:]
        x = x + (jax.nn.silu(gate) * val) @ w_down[l]
    return x
```

```python
import math
import numpy as np
import concourse.bass as bass
import concourse.mybir as mybir
from concourse.bass_utils import run_bass_kernel_spmd

F32 = mybir.dt.float32
BF16 = mybir.dt.bfloat16
AF = mybir.ActivationFunctionType
ALU = mybir.AluOpType

D = 1024
S = 4096
H = 8
DFF = 2816
NFC = DFF // 128
EPS = 1e-6
LAM_INIT = 0.8 - 0.6 * math.exp(0.0)
NT = 32
NTO = 17
TOWN = NTO * 128
TOTH = S - TOWN
VW = 130
SLOPES = [2.0 ** (-(h + 1)) for h in range(H)]
NBL = 18
NBR = 32
USE_V_FILLERS = False
SKIP_T = 192.0


class _Sem:
    def __init__(self, handle):
        self.handle = handle
        self.count = 0


class _Op:
    __slots__ = ("eng", "fn", "deps", "sem", "val", "signal", "is_dma", "idx")

    def __init__(self, eng, fn, is_dma):
        self.eng = eng
        self.fn = fn
        self.deps = []
        self.sem = None
        self.val = 0
        self.signal = False
        self.is_dma = is_dma


class Prog:
    ENGS = ("pe", "act", "dve", "pool", "sp")

    def __init__(self, esems):
        self.esems = esems
        self.ops = {e: [] for e in self.ENGS}
        self.last_w = {}
        self.readers = {}
        self.pending = {e: [] for e in self.ENGS}
        self.dsems = []

    def dsem(self, handle, group=False):
        s = _Sem(handle)
        s.group = group
        s.gops = []
        self.dsems.append(s)
        return s

    def add(self, eng, fn, reads=(), writes=(), dsem=None):
        op = _Op(eng, fn, dsem is not None)
        deps = []
        for k in reads:
            w = self.last_w.get(k)
            if w is not None:
                deps.append(w)
        for k in writes:
            w = self.last_w.get(k)
            if w is not None:
                deps.append(w)
            deps.extend(self.readers.get(k, ()))
        deps.extend(self.pending[eng])
        self.pending[eng] = []
        latest = {}
        seen = set()
        for d in deps:
            if d is op or id(d) in seen:
                continue
            seen.add(id(d))
            if d.eng == eng and eng == "pe" and not d.is_dma:
                continue
            if d.is_dma:
                op.deps.append(d)
            else:
                cur = latest.get(d.eng)
                if cur is None or d.idx > cur.idx:
                    latest[d.eng] = d
        op.deps.extend(latest.values())
        for k in reads:
            self.readers.setdefault(k, []).append(op)
        for k in writes:
            self.last_w[k] = op
            self.readers[k] = []
        if dsem is not None:
            dsem.count += 1
            op.sem = dsem
            op.val = 16 * dsem.count
            op.signal = True
            dsem.last = op
            dsem.gops.append(op)
        else:
            op.sem = self.esems[eng]
        op.idx = len(self.ops[eng])
        self.ops[eng].append(op)
        return op

    def barrier(self):
        lasts = []
        for e in self.ENGS:
            for op in reversed(self.ops[e]):
                if not op.is_dma:
                    lasts.append(op)
                    break
        for s in self.dsems:
            if getattr(s, "last", None) is not None:
                lasts.append(s.last)
        for e in self.ENGS:
            self.pending[e] = list(lasts)

    def emit(self, block, final_waits):
        for s in self.dsems:
            if s.group:
                for op in s.gops:
                    op.val = 16 * s.count
        for e in self.ENGS:
            for op in self.ops[e]:
                for d in op.deps:
                    d.signal = True
        for e in self.ENGS:
            c = 0
            for op in self.ops[e]:
                if not op.is_dma and op.signal:
                    c += 1
                    op.val = c

        def run(name, eng):
            waited = {}
            for op in self.ops[name]:
                need = {}
                for d in op.deps:
                    if need.get(d.sem, 0) < d.val:
                        need[d.sem] = d.val
                for s, v in need.items():
                    if waited.get(s, 0) < v:
                        eng.wait_ge(s.handle, v)
                        waited[s] = v
                ins = op.fn(eng)
                if op.is_dma:
                    ins.then_inc(op.sem.handle, 16)
                elif op.signal:
                    ins.then_inc(op.sem.handle, 1)
            if name == "sp":
                for s in final_waits:
                    eng.wait_ge(s.handle, 16 * s.count)

        @block.tensor
        def _(t):
            run("pe", t)

        @block.scalar
        def _(a):
            run("act", a)

        @block.vector
        def _(v):
            run("dve", v)

        @block.gpsimd
        def _(g):
            run("pool", g)

        @block.sync
        def _(s):
            run("sp", s)


class Arena:
    def __init__(self, t, nwords):
        self.t = t
        self.n = nwords

    def f32(self, off, n):
        assert off + n <= self.n, (off, n, self.n)
        return self.t[:, off:off + n]

    def bf16(self, off, n):
        w = (n + 1) // 2
        assert off + w <= self.n, (off, w, self.n)
        return self.t[:, off:off + w].bitcast(BF16)


def I(method, *args, **kwargs):
    def fn(e):
        return getattr(e, method)(*args, **kwargs)
    return fn


def _blocks(total, width=512):
    out = []
    s = 0
    while s < total:
        w = min(width, total - s)
        out.append((s, w))
        s += w
    return out


def build(debug=None, stop_after=None):
    debug = debug or ()
    nc = bass.Bass("TRN2", target_bir_lowering=False)

    def din(name, shape):
        return nc.dram_tensor(name, list(shape), F32, kind="ExternalInput").ap()

    xs = din("xs", [S, D])
    c_ident = din("c_ident", [128, 128])
    c_blk = din("c_blk", [128, 128])
    c_T = din("c_T", [128, 896])
    c_bias = din("c_bias", [128, H * (NBL + NBR)])
    c_bq = din("c_bq", [128, H * 8])
    c_vec = din("c_vec", [128, 64])
    c_lam = din("c_lam", [128, 256])
    c_sgb = din("c_sgb", [128, 1024])
    c_conv = din("c_conv", [128, 2 * NFC * 4])
    w_qkv = din("w_qkv", [H, 128, 8 * 384])
    w_us = din("w_us", [8, 128, 8 * 256])
    w_ab = din("w_ab", [8, 128, 8 * 512])
    w_out = din("w_out", [128, 8 * 1024])
    w_up = din("w_up", [NFC, 128, 8 * 256])
    w_dn = din("w_dn", [2, 128, NFC * 512])
    sg_wT = din("sg_wT", [128, 8 * 128])
    out = nc.dram_tensor("out", [2048, D], F32, kind="ExternalOutput").ap()
    dbg = {}
    dbg_shapes = {"hT": [128, 8 * TOWN], "onT": [128, 8 * TOWN], "obT": [128, 8 * TOWN],
                  "mT": [128, 8 * TOWN], "xmid": [128, 16 * 1024], "KT": [128, S],
                  "QT": [128, TOWN], "V": [128, NT * VW], "h2T": [128, 8 * TOWN]}
    for k in debug:
        dbg[k] = nc.dram_tensor("dbg_" + k, dbg_shapes[k], F32, kind="ExternalOutput").ap()

    NW = 53200
    import contextlib
    with contextlib.ExitStack() as es:
        arena_t = es.enter_context(nc.sbuf_tensor("arena", [128, NW], F32))
        ps_all = es.enter_context(nc.psum_tensor("ps_all", [128, 8, 512], F32))
        sem_names = ["pe", "act", "dve", "pool"]
        esems = {n: _Sem(es.enter_context(nc.semaphore("s_" + n))) for n in sem_names}
        esems["sp"] = _Sem(es.enter_context(nc.semaphore("s_sp")))
        P = Prog(esems)

        def new_dsem(name, group=False):
            return P.dsem(es.enter_context(nc.semaphore("d_" + name)), group=group)

        ar = Arena(arena_t, NW)

        o = 0
        def take(nwords):
            nonlocal o
            r = o
            o += nwords
            return r
        ident = ar.bf16(take(64), 128)
        blk = ar.bf16(take(64), 128)
        Tm = ar.f32(take(896), 896)
        biasT = ar.f32(take(H * (NBL + NBR)), H * (NBL + NBR))
        bq = ar.f32(take(H * 8), H * 8)
        vec = ar.f32(take(64), 64)
        lamv = ar.f32(take(256), 256)
        sgb = ar.f32(take(1024), 1024)
        convp = ar.f32(take(2 * NFC * 4), 2 * NFC * 4)
        sgw = ar.bf16(take(512), 1024)
        small = ar.f32(take(256), 256)
        CONST_END = o
        R0 = CONST_END
        KB = 256
        hT_own_off = R0
        onT_off = R0 + 34 * KB
        hT_oth_off = R0 + 68 * KB
        obT_off = hT_oth_off
        TMP = R0 + 102 * KB
        hT_own = ar.bf16(hT_own_off, 8 * TOWN).rearrange("p (c t) -> p c t", c=8)
        hT_oth = ar.bf16(hT_oth_off, 8 * TOTH).rearrange("p (c t) -> p c t", c=8)
        onT = ar.bf16(onT_off, 8 * TOWN).rearrange("p (c t) -> p c t", c=8)
        obT = ar.bf16(obT_off, 8 * TOWN).rearrange("p (c t) -> p c t", c=8)

        V_GMIX, V_GQ, V_GK, V_SUB, V_SGN, V_GFFN, V_NEGH = 0, 8, 9, 10, 11, 19, 27
        gmix = vec[:, V_GMIX:V_GMIX + 8]
        gffn = vec[:, V_GFFN:V_GFFN + 8]
        sgn = vec[:, V_SGN:V_SGN + 8]
        negh = vec[:, V_NEGH:V_NEGH + 1]
        gqs = small[:, 0:1]
        subs = small[:, 1:2]
        neglam = small[:, 2:3]
        d12 = small[:, 3:5]
        e12 = small[:, 5:7]
        ljunk = small[:, 8:72]

        dconst = new_dsem("const", group=True)
        dconst2 = new_dsem("const2", group=True)
        out_sems = []

        def dma_sp(out_ap, in_ap, sem, reads=(), writes=()):
            return P.add("sp", I("dma_start", out=out_ap, in_=in_ap), reads, writes, dsem=sem)

        def dma_pool(out_ap, in_ap, sem, reads=(), writes=()):
            return P.add("pool", I("dma_start", out=out_ap, in_=in_ap), reads, writes, dsem=sem)

        dma_pool(ident, c_ident, dconst2, writes=["ident"])
        dma_pool(blk, c_blk, dconst2, writes=["blk"])
        dma_pool(sgw, sg_wT, dconst2, writes=["sgw"])
        dma_sp(Tm, c_T, dconst, writes=["Tm"])
        dma_sp(biasT, c_bias, dconst, writes=["biasT"])
        dma_sp(bq, c_bq, dconst, writes=["bq"])
        dma_sp(vec, c_vec, dconst, writes=["vec"])
        dma_sp(lamv, c_lam, dconst, writes=["lamv"])
        dma_sp(sgb, c_sgb, dconst, writes=["sgb"])
        dma_sp(convp, c_conv, dconst, writes=["convp"])

        P.add("dve", I("tensor_scalar", out=gqs, in0=vec[:, V_GQ:V_GQ + 1], scalar1=0.125,
                                               scalar2=None, op0=ALU.mult), ["vec"], ["gqs"])
        P.add("dve", I("tensor_scalar", out=subs, in0=vec[:, V_SUB:V_SUB + 1],
                                               scalar1=float(1.0 - LAM_INIT), scalar2=None, op0=ALU.mult),
              ["vec"], ["subs"])
        P.add("dve", I("scalar_tensor_tensor", out=ljunk, in0=lamv[:, 0:64], scalar=1.0,
                                                      in1=lamv[:, 64:128], op0=ALU.mult, op1=ALU.mult,
                                                      accum_out=d12[:, 0:1]), ["lamv"], ["d1", "ljunk"])
        P.add("dve", I("scalar_tensor_tensor", out=ljunk, in0=lamv[:, 128:192], scalar=1.0,
                                                      in1=lamv[:, 192:256], op0=ALU.mult, op1=ALU.mult,
                                                      accum_out=d12[:, 1:2]), ["lamv"], ["d2", "ljunk"])
        P.add("act", I("activation", out=e12, in_=d12, func=AF.Exp), ["d1", "d2"], ["e12"])
        P.add("dve", I("tensor_tensor", out=neglam, in0=e12[:, 1:2], in1=e12[:, 0:1], op=ALU.subtract),
              ["e12"], ["nl0"])
        P.add("dve", I("tensor_scalar", out=neglam, in0=neglam, scalar1=float(-LAM_INIT), scalar2=None,
                                               op0=ALU.add), ["nl0"], ["neglam"])

        def psf(b):
            return ps_all[:, b, :]

        def psb(b):
            return ps_all[:, b, :].bitcast(BF16)

        bank_ctr = [0]

        def next_bank(lo=0, hi=8):
            b = lo + bank_ctr[0] % (hi - lo)
            bank_ctr[0] += 1
            return b

        def pskey(b):
            return ("ps", b)

        t0 = TMP
        NXT = 4
        xt_bufs = [ar.f32(t0 + i * 1024, 1024) for i in range(NXT)]
        t0 += NXT * 1024
        xn_bufs = [ar.bf16(t0 + i * 512, 1024) for i in range(2)]
        sqj = ar.bf16(t0 + 1024, 1024)
        stat = ar.f32(t0 + 1536, 128)
        PH_A = t0 + 1536 + 128
        xt_sems = [new_dsem("xt%d" % i) for i in range(NXT)]
        nctr = [0]

        def norm_tile(src_key, src_ap, gvec, dstT, col0, dst_key):
            i = nctr[0]
            nctr[0] += 1
            sc = (i % 32) * 4
            ss, ms, rs = stat[:, sc:sc + 1], stat[:, sc + 1:sc + 2], stat[:, sc + 2:sc + 3]
            xn = xn_bufs[i % 2]
            xnk = ("xn", i % 2)
            hold = {}

            def st_a():
                P.add("act", I("activation", out=sqj, in_=src_ap, func=AF.Square, accum_out=ss),
                      [src_key], [("ss", sc), "sqj"])

            def st_b():
                P.add("dve", I("tensor_scalar", out=ms, in0=ss, scalar1=1.0 / D, scalar2=EPS,
                               op0=ALU.mult, op1=ALU.add), [("ss", sc)], [("ms", sc)])
                P.add("pool", I("tensor_tensor", out=rs, in0=ms, in1=negh, op=ALU.pow),
                      [("ms", sc), "vec"], [("rs", sc)])

            def st_c():
                P.add("dve", I("tensor_scalar", out=xn, in0=src_ap, scalar1=rs, scalar2=None, op0=ALU.mult),
                      [src_key, ("rs", sc)], [xnk])
                b = next_bank()
                hold["b"] = b
                pv = psb(b)
                for c in range(8):
                    P.add("pe", I("transpose", out=pv[:, c * 128:(c + 1) * 128], in_=xn[:, c * 128:(c + 1) * 128],
                                  identity=ident), [xnk, "ident"], [pskey(b)])

            def st_d():
                pv = psb(hold["b"])
                P.add("dve", I("tensor_tensor", out=dstT[:, :, col0:col0 + 128],
                               in0=pv.rearrange("p (c t) -> p c t", c=8),
                               in1=gvec.unsqueeze(2).to_broadcast([128, 8, 128]), op=ALU.mult),
                      [pskey(hold["b"]), "vec"], [dst_key])

            return (st_a, st_b, st_c, st_d)

        if "onT" in debug:
            P.add("pool", I("memset", ar.bf16(onT_off, 8 * TOWN), 0.0), [], [("onT", hh, qq) for hh in range(H) for qq in range(NTO)])
        p0 = []
        for tt in range(NT):
            bi = tt % NXT
            xt = xt_bufs[bi]
            if tt < NTO:
                st = norm_tile(("xt", bi), xt, gmix, hT_own, tt * 128, ("hT", tt))
            else:
                st = norm_tile(("xt", bi), xt, gmix, hT_oth, (tt - NTO) * 128, ("hT", tt))
            p0.append((tt, bi, xt, st))
        for n in range(NT + 3):
            for si in range(4):
                k_ = n - si
                if 0 <= k_ < NT:
                    tt, bi, xt, st = p0[k_]
                    if si == 0:
                        dma_sp(xt, xs[tt * 128:(tt + 1) * 128, :], xt_sems[bi], writes=[("xt", bi)])
                    st[si]()

        def hT_block(tok0, w):
            if tok0 < TOWN:
                assert tok0 + w <= TOWN
                return hT_own, tok0, [("hT", t) for t in range(tok0 // 128, (tok0 + w + 127) // 128)]
            return hT_oth, tok0 - TOWN, [("hT", t) for t in range(tok0 // 128, (tok0 + w + 127) // 128)]

        own_blocks = _blocks(TOWN)
        all_blocks = own_blocks + [(TOWN + s, w) for s, w in _blocks(TOTH)]

        def dump(name, src_flat):
            P.barrier()
            s_ = new_dsem("dbg_" + name); out_sems.append(s_)
            dma_pool(dbg[name], src_flat, s_)
            P.barrier()

        if "hT" in debug:
            dump("hT", ar.bf16(hT_own_off, 8 * TOWN))

        a0 = PH_A
        KT = ar.bf16(a0, S); a0 += S // 2
        Va_flats = [ar.bf16(a0 + i * ((NT * VW) // 2), NT * VW) for i in range(2)]
        Va_bufs = [f_.rearrange("p (t v) -> p t v", v=VW) for f_ in Va_flats]
        Va_flat = Va_flats[0]
        a0 += NT * VW
        QT = ar.bf16(a0, TOWN); a0 += TOWN // 2
        wq_bufs = [ar.bf16(a0 + i * 1536, 3072).rearrange("p (c n) -> p c n", c=8) for i in range(2)]
        a0 += 3072
        wq_sems = [new_dsem("wq%d" % i) for i in range(2)]
        sq_bufs = [ar.bf16(a0 + i * 256, 512) for i in range(2)]; a0 += 512
        sd_bufs = [ar.f32(a0 + i * 512, 512) for i in range(2)]; a0 += 1024
        rs_bufs = [ar.f32(a0 + i * 512, 512) for i in range(2)]; a0 += 1024
        al = TMP
        NPB = 3
        Pb = [ar.bf16(al + i * 512, 1024).rearrange("p (m n) -> p m n", m=2) for i in range(NPB)]; al += NPB * 512
        sDb = [ar.f32(al + i * 1024, 1024).rearrange("p (m n) -> p m n", m=2) for i in range(2)]; al += 2048
        tots = [ar.f32(al + i * 8 * VW, 8 * VW).rearrange("p (i m v) -> p i m v", i=4, m=2) for i in range(2)]
        al += 16 * VW
        assert al <= PH_A
        osb = [ar.f32(a0 + i * 128, 128) for i in range(2)]; a0 += 256
        ojunk = ar.f32(a0, 128); a0 += 128
        o2b = [ar.f32(a0 + i * 128, 128) for i in range(2)]; a0 += 256
        onb = [ar.bf16(a0 + i * 64, 128) for i in range(4)]; a0 += 256
        est = ar.f32(a0, 64); a0 += 64
        Mh = ar.f32(a0, 896); a0 += 896
        assert a0 <= NW, a0

        for Va_ in Va_bufs:
            P.add("pool", I("memset", Va_[:, :, 128:129], 1.0), [], ["Vones"])
            P.add("pool", I("memset", Va_[:, :, 129:130], 0.0), [], ["Vpad"])

        pctr = [0]

        def qk_project(h, wq, wkey, col_lo, blocks, gvecap, gkey, dst, dst_key_fn):
            steps = []
            for (s0, w) in blocks:
                steps.append(_qk_block(wq, wkey, col_lo, s0, w, gvecap, gkey, dst, dst_key_fn))
            for n in range(len(steps) + 1):
                if n < len(steps):
                    steps[n][0]()
                if n >= 1:
                    steps[n - 1][1]()

        def _qk_block(wq, wkey, col_lo, s0, w, gvecap, gkey, dst, dst_key_fn):
            src, so, hkeys = hT_block(s0, w)
            i = pctr[0]; pctr[0] += 1
            bA = pbank[0] % 4; bB = 4 + (pbank[0] % 4); pbank[0] += 1
            pA = psf(bA)[:, 0:w]; pB = psf(bB)[:, 0:w]
            sq = sq_bufs[i % 2][:, 0:w]; sd = sd_bufs[i % 2][:, 0:w]

            def stage1():
                for c in range(8):
                    P.add("pe", I("matmul", pA, lhsT=wq[:, c, col_lo:col_lo + 128], rhs=src[:, c, so:so + w],
                                  start=(c == 0), stop=(c == 7)), hkeys + [wkey], [pskey(bA)])
                P.add("act", I("activation", out=sq, in_=pA, func=AF.Square), [pskey(bA)], [("sq", i % 2)])

            def stage2():
                P.add("pe", I("matmul", pB, lhsT=blk, rhs=sq, start=True, stop=True), [("sq", i % 2), "blk"], [pskey(bB)])
                rs = rs_bufs[i % 2][:, 0:w]
                P.add("act", I("activation", out=sd, in_=pB, func=AF.Ln, bias=EPS_AP, scale=1.0),
                      [pskey(bB), "epsap"], [("sd", i % 2)])
                P.add("act", I("activation", out=rs, in_=sd, func=AF.Exp, scale=-0.5),
                      [("sd", i % 2)], [("rsb", i % 2)])
                P.add("dve", I("scalar_tensor_tensor", out=dst[:, s0:s0 + w], in0=pA, scalar=gvecap, in1=rs,
                               op0=ALU.mult, op1=ALU.mult), [pskey(bA), ("rsb", i % 2), gkey], [dst_key_fn(s0)])

            return (stage1, stage2)

        pbank = [0]

        EPS_AP = small[:, 7:8]
        P.add("pool", I("memset", EPS_AP, EPS), [], ["epsap"])

        gctr = [0]
        dctr = [0]
        ectr = [0]
        tpctr = [0]
        psflat = ps_all.rearrange("p b n -> p (b n)")

        def acc_off(a):
            return (4 + a // 3) * 512 + (a % 3) * VW

        def acc_tile(a):
            o_ = acc_off(a)
            return psflat[:, o_:o_ + VW]

        def acc_pair(i):
            o0 = acc_off(2 * i)
            st = acc_off(2 * i + 1) - o0
            return psflat[:, o0:o0 + 2 * st].rearrange("p (m s) -> p m s", m=2)[:, :, 0:VW]

        def acc_bank(a):
            return 4 + a // 3

        blkctr = [0]
        pending_combine = []

        def make_jobs(h, q0, W):
            slope = SLOPES[h]
            nq = W // 128
            tb = blkctr[0] % 2; blkctr[0] += 1

            def far(kt):
                k_lo, k_hi = kt * 128, kt * 128 + 127
                dist = (q0 - k_hi) if k_hi < q0 else (k_lo - (q0 + W - 1))
                return slope * dist >= SKIP_T
            Lt = [kt for kt in range(NT) if (kt + 1) * 128 <= q0 and not far(kt)]
            Dt = [kt for kt in range(NT) if q0 <= kt * 128 < q0 + W]
            Rt = [kt for kt in range(NT) if kt * 128 >= q0 + W and not far(kt)]
            phases = [(ph, tl) for ph, tl in (("L", Lt), ("D", Dt), ("R", Rt)) if tl]
            jobs = []
            for pi, (phase, tiles) in enumerate(phases):
                for n, kt in enumerate(tiles):
                    buf = gctr[0] % 2; pbuf = gctr[0] % NPB; gctr[0] += 1
                    jobs.append(_make_job(h, q0, W, nq, slope, phase, pi == 0, pi == len(phases) - 1,
                                          n, len(tiles), kt, buf, tb, pbuf))
            return jobs

        def _make_job(h, q0, W, nq, slope, phase, first_phase, last_phase, n, ntiles, kt, buf, tb, pbuf):
            b0 = 2 * buf
            Skey = ("S", buf)
            Pk = ("P", pbuf)
            tot = tots[tb]

            def S_fn():
                P.add("pe", I("matmul", ps_all[:, b0, 0:W], lhsT=KT[0:64, kt * 128:(kt + 1) * 128],
                              rhs=QT[0:64, q0:q0 + W], start=True, stop=True, tile_position=(0, 0)),
                      ["KT", "QT"], [pskey(b0), Skey])
                P.add("pe", I("matmul", ps_all[:, b0 + 1, 0:W], lhsT=KT[64:128, kt * 128:(kt + 1) * 128],
                              rhs=QT[64:128, q0:q0 + W], start=True, stop=True, tile_position=(64, 0)),
                      ["KT", "QT"], [pskey(b0 + 1), Skey])

            def combine():
                for i in range(nq):
                    accv = acc_pair(i)
                    tv = tot[:, i, :, :]
                    tk = ("tot", tb, i)
                    if phase == "L":
                        sc = bq[:, h * 8 + i: h * 8 + i + 1]
                    elif phase == "R":
                        ii = i if W == 512 else 3
                        sc = bq[:, h * 8 + 4 + ii: h * 8 + 4 + ii + 1]
                    else:
                        sc = None
                    rk = [pskey(acc_bank(2 * i)), pskey(acc_bank(2 * i + 1)), ("acc", 2 * i), ("acc", 2 * i + 1), "bq"]
                    if first_phase:
                        if sc is None:
                            P.add("dve", I("tensor_copy", out=tv, in_=accv), rk, [tk])
                        else:
                            P.add("dve", I("tensor_scalar", out=tv, in0=accv, scalar1=sc, scalar2=None,
                                           op0=ALU.mult), rk, [tk])
                    else:
                        if sc is None:
                            P.add("dve", I("tensor_tensor", out=tv, in0=accv, in1=tv, op=ALU.add), rk + [tk], [tk])
                        else:
                            P.add("dve", I("scalar_tensor_tensor", out=tv, in0=accv, scalar=sc, in1=tv,
                                           op0=ALU.mult, op1=ALU.add), rk + [tk], [tk])

            def act_fn():
                Sin = ps_all[:, b0:b0 + 2, 0:W]
                pout = Pb[pbuf][:, :, 0:W]
                if phase == "D":
                    jD = kt - q0 // 128
                    db = dctr[0] % 2; dctr[0] += 1
                    sd_ = sDb[db][:, :, 0:W]
                    msl = Mh[:, 384 - 128 * jD: 384 - 128 * jD + W]
                    P.add("act", I("activation", out=sd_, in_=Sin, func=AF.Exp),
                          [Skey, pskey(b0), pskey(b0 + 1)], [("sD", db)])
                    P.add("dve", I("tensor_tensor", out=pout, in0=sd_,
                                   in1=msl.unsqueeze(1).to_broadcast([128, 2, W]), op=ALU.mult),
                          [("sD", db), "Mh"], [(Pk, 0), (Pk, 1)])
                else:
                    if phase == "L":
                        j = (q0 - kt * 128) // 128
                        col = h * (NBL + NBR) + j
                    else:
                        j = (kt * 128 - q0 - W) // 128
                        col = h * (NBL + NBR) + NBL + j
                    bap = biasT[:, col:col + 1]
                    P.add("act", I("activation", out=pout, in_=Sin, func=AF.Exp, bias=bap, scale=1.0),
                          [Skey, pskey(b0), pskey(b0 + 1), "biasT"], [(Pk, 0), (Pk, 1)])
            def av_fn():
                while pending_combine:
                    pending_combine.pop(0)()
                for i in range(nq):
                    for m in range(2):
                        a = 2 * i + m
                        P.add("pe", I("matmul", acc_tile(a), lhsT=Pb[pbuf][:, m, i * 128:(i + 1) * 128],
                                      rhs=Va_bufs[h % 2][:, kt, :], start=(n == 0 and a % 3 == 0),
                                      stop=(n == ntiles - 1), skip_group_check=True),
                              [(Pk, m), ("V", h % 2), "Vones", "Vpad"], [pskey(acc_bank(a)), ("acc", a)])
                if n == ntiles - 1:
                    pending_combine.append(combine)

            deferred = []
            if n == ntiles - 1 and last_phase:
                for i in range(nq):
                    k = ectr[0]; ectr[0] += 1
                    qt = q0 // 128 + i
                    deferred.append((2 + i, _epi_a(tot, tb, i, k), "a"))
                    deferred.append((4 + i, _epi_b(k), "b"))
                    deferred.append((6 + i, _epi_tail(h, k, qt), "t"))
            is_last_D = (phase == "D" and n == ntiles - 1)
            return (S_fn, act_fn, av_fn, deferred, is_last_D)

        def _epi_slots(k):
            eb = (k % 8) * 8
            return (est[:, eb:eb + 2], est[:, eb + 2:eb + 3], est[:, eb + 3:eb + 4], est[:, eb + 4:eb + 5],
                    est[:, eb + 5:eb + 6], ("est", k % 8))

        def _epi_a(tot, tb, i, k):
            rz, nl2, ss, ms, rs, ek = _epi_slots(k)
            o1 = osb[0]; o2 = o2b[k % 2]
            tk = ("tot", tb, i)

            def fn():
                P.add("dve", I("reciprocal", out=rz, in_=tot[:, i, :, 128]), [tk], [ek])
                P.add("dve", I("tensor_scalar", out=nl2, in0=rz[:, 1:2], scalar1=neglam, scalar2=None,
                               op0=ALU.mult), [ek, "neglam"], [(ek, "nl2")])
                P.add("dve", I("tensor_scalar", out=o1, in0=tot[:, i, 0, 0:128], scalar1=rz[:, 0:1],
                               scalar2=None, op0=ALU.mult), [tk, ek], ["o1"])
                P.add("dve", I("scalar_tensor_tensor", out=o2, in0=tot[:, i, 1, 0:128], scalar=nl2, in1=o1,
                               op0=ALU.mult, op1=ALU.add), [tk, (ek, "nl2"), "o1"], [("o2", k % 2)])
                P.add("dve", I("scalar_tensor_tensor", out=ojunk, in0=o2, scalar=1.0, in1=o2, op0=ALU.mult,
                               op1=ALU.mult, accum_out=ss), [("o2", k % 2)], [(ek, "ss"), "ojunk"])
                P.add("dve", I("tensor_scalar", out=ms, in0=ss, scalar1=1.0 / 128, scalar2=EPS, op0=ALU.mult,
                               op1=ALU.add), [(ek, "ss")], [(ek, "ms")])
                P.add("pool", I("tensor_tensor", out=rs, in0=ms, in1=negh, op=ALU.pow),
                      [(ek, "ms"), "vec"], [(ek, "rs")])
            return fn

        def _epi_b(k):
            rz, nl2, ss, ms, rs, ek = _epi_slots(k)

            def fn():
                P.add("dve", I("tensor_scalar", out=onb[k % 4], in0=o2b[k % 2], scalar1=rs, scalar2=None,
                               op0=ALU.mult), [("o2", k % 2), (ek, "rs")], [("on", k % 4)])
            return fn

        def _epi_tail(h, k, qt):
            def fn():
                sl = tpctr[0] % 8; tpctr[0] += 1
                pv = psb(7)[:, sl * 128:(sl + 1) * 128]
                P.add("pe", I("transpose", out=pv, in_=onb[k % 4], identity=ident),
                      [("on", k % 4), "ident"], [pskey(7)])
                P.add("dve", I("tensor_scalar", out=onT[:, h, qt * 128:(qt + 1) * 128], in0=pv, scalar1=subs,
                               scalar2=None, op0=ALU.mult), [pskey(7), "subs"], [("onT", h, qt)])
            return fn

        def run_pipeline(jobs, fillers=(), fill_pos=()):
            pend = []
            fillers = list(fillers)
            nfill = [0]
            N_ = len(jobs)
            for n in range(N_ + 2):
                if n >= 2:
                    jobs[n - 2][1]()
                if n < N_:
                    jobs[n][0]()
                if n >= 2:
                    jobs[n - 2][2]()
                    if jobs[n - 2][4]:
                        for it in pend:
                            it[2] = True
                    for (d, fn, kind) in jobs[n - 2][3]:
                        pend.append([d, fn, False, kind])
                    if (n - 2) in fill_pos and fillers:
                        fillers.pop(0)()
                        nfill[0] += 1
                group_open = (nfill[0] % 4) != 0
                keep = []
                for it in pend:
                    if it[2]:
                        it[0] -= 1
                    if it[2] and it[0] <= 0 and not (it[3] == "t" and group_open):
                        it[1]()
                    else:
                        keep.append(it)
                pend = keep
            while pending_combine:
                pending_combine.pop(0)()
            while (nfill[0] % 4) != 0 and fillers:
                fillers.pop(0)()
                nfill[0] += 1
            pend.sort(key=lambda it: it[0])
            for it in pend:
                it[1]()
            for f_ in fillers:
                f_()

        def v_group(h, g4, bank, eng):
            wq = wq_bufs[h % 2]
            wkey = ("wq", h % 2)
            Va_ = Va_bufs[h % 2]

            def part(j):
                def fn():
                    tt = g4 * 4 + j
                    src, so, hkeys = hT_block(tt * 128, 128)
                    for c in range(8):
                        P.add("pe", I("matmul", psf(bank)[:, j * 128:(j + 1) * 128], lhsT=src[:, c, so:so + 128],
                                      rhs=wq[:, c, 256:384], start=(c == 0), stop=(c == 7), skip_group_check=True),
                              hkeys + [wkey], [pskey(bank)])
                    if j != 3:
                        return
                    src_v = psf(bank).rearrange("p (j d) -> p j d", j=4)
                    dst_v = Va_[:, g4 * 4:(g4 + 1) * 4, 0:128]
                    if eng == "act":
                        P.add("act", I("activation", out=dst_v, in_=src_v, func=AF.Copy), [pskey(bank)], [("V", h % 2)])
                    else:
                        P.add("dve", I("tensor_copy", out=dst_v, in_=src_v), [pskey(bank)], [("V", h % 2)])
                return fn
            return [part(j) for j in range(4)]

        def load_wq(h):
            dma_pool(wq_bufs[h % 2].rearrange("p c n -> p (c n)"), w_qkv[h], wq_sems[h % 2], writes=[("wq", h % 2)])

        load_wq(0)
        for h in range(H):
            wq = wq_bufs[h % 2]
            wkey = ("wq", h % 2)
            qk_project(h, wq, wkey, 128, all_blocks, vec[:, V_GK:V_GK + 1], "vec", KT, lambda s0: "KT")
            qk_project(h, wq, wkey, 0, own_blocks, gqs, "gqs", QT, lambda s0: "QT")
            if h == 0 or not USE_V_FILLERS:
                for g4 in range(NT // 4):
                    for f_ in v_group(h, g4, next_bank(), "act"):
                        f_()
            if h + 1 < H:
                load_wq(h + 1)
            if h == 0:
                for nm, src_ap in (("KT", KT), ("QT", QT), ("V", Va_flat)):
                    if nm in debug:
                        dump(nm, src_ap)
            if stop_after == "proj0":
                break
            P.add("act", I("activation", out=Mh, in_=Tm, func=AF.Exp, scale=float(SLOPES[h])), ["Tm"], ["Mh"])
            jobs = []
            fill_pos = []
            for (q0, W) in own_blocks:
                jb = make_jobs(h, q0, W)
                base = len(jobs)
                if len(jb) >= 22:
                    fill_pos += [base + 10 + i_ for i_ in range(4)] + [base + 16 + i_ for i_ in range(4)]
                elif len(jb) >= 15:
                    fill_pos += [base + 10 + i_ for i_ in range(4)]
                jobs.extend(jb)
            fillers = []
            if USE_V_FILLERS and h + 1 < H and stop_after != "head0":
                fillers = [f_ for g4 in range(NT // 4) for f_ in v_group(h + 1, g4, 7, "dve")]
            run_pipeline(jobs, fillers, set(fill_pos))
            if stop_after == "head0":
                break

        if "onT" in debug:
            dump("onT", ar.bf16(onT_off, 8 * TOWN))

        done = stop_after in ("proj0", "head0", "attn")
        if not done:
            build_rest(nc, P, ar, ps_all, locals())

        with nc.Block() as block:
            P.emit(block, out_sems)
    return nc


def build_rest(nc, P, ar, ps_all, L):
    names = ("TMP hT_own onT obT sgw sgb sgn vec negh ident gffn convp xs w_us w_ab w_out w_up w_dn out debug dbg "
             "new_dsem dma_sp dma_pool psf psb pskey next_bank own_blocks small R0 KB out_sems dump obT_off").split()
    (TMP, hT_own, onT, obT, sgw, sgb, sgn, vec, negh, ident, gffn, convp, xs, w_us, w_ab, w_out, w_up, w_dn, out,
     debug, dbg, new_dsem, dma_sp, dma_pool, psf, psb, pskey, next_bank, own_blocks, small, R0, KB, out_sems,
     dump, obT_off) = [L[n] for n in names]
    NW = ar.n
    stop_after = L["stop_after"]

    P.barrier()
    b0 = TMP
    mT = ar.bf16(b0, 8 * TOWN).rearrange("p (c t) -> p c t", c=8); b0 += 4 * TOWN
    uT_bufs = [ar.f32(b0 + i * TOWN, TOWN) for i in range(2)]; b0 += 2 * TOWN
    wus_bufs = [ar.bf16(b0 + i * 1024, 2048).rearrange("p (c n) -> p c n", c=8) for i in range(2)]; b0 += 2048
    wus_sems = [new_dsem("wus%d" % i) for i in range(2)]
    gl_bufs = [ar.f32(b0 + i * 128, 128) for i in range(4)]; b0 += 512
    gjunk = ar.f32(b0, 128); b0 += 128
    svh_bufs = [ar.bf16(b0 + i * 64, 128) for i in range(4)]; b0 += 256
    mtmp = ar.f32(b0, 512); b0 += 512
    gst = ar.f32(b0, 64); b0 += 64
    B2_OFF = b0
    sctr = [0]
    mixctr = [0]
    svbank = [0]

    def _sgu_tile(g, wus, wkey, t4, tiles, j, tt, bm):
        uT = uT_bufs[g % 2]
        k = sctr[0]; sctr[0] += 1
        b = svbank[0] % 6; svbank[0] += 1
        pv = psf(b)[:, 0:128]
        gl = gl_bufs[k % 4]; svh = svh_bufs[k % 4]
        sb_ = (k % 8) * 4
        ss, ms, rs = gst[:, sb_:sb_ + 1], gst[:, sb_ + 1:sb_ + 2], gst[:, sb_ + 2:sb_ + 3]
        gk = ("gst", k % 8)

        def proj():
            for c in range(8):
                P.add("pe", I("matmul", pv, lhsT=hT_own[:, c, tt * 128:(tt + 1) * 128], rhs=wus[:, c, 128:256],
                              start=(c == 0), stop=(c == 7)), [("hT", tt), wkey], [pskey(b)])
            P.add("act", I("activation", out=gl, in_=pv, func=AF.Gelu), [pskey(b)], [("gl", k % 4)])
            P.add("dve", I("scalar_tensor_tensor", out=gjunk, in0=gl, scalar=1.0, in1=gl, op0=ALU.mult,
                           op1=ALU.mult, accum_out=ss), [("gl", k % 4)], [(gk, "ss"), "gjunk"])
            P.add("dve", I("tensor_scalar", out=ms, in0=ss, scalar1=1.0 / 128, scalar2=EPS, op0=ALU.mult,
                           op1=ALU.add), [(gk, "ss")], [(gk, "ms")])
            P.add("pool", I("tensor_tensor", out=rs, in0=ms, in1=negh, op=ALU.pow), [(gk, "ms"), "vec"], [(gk, "rs")])

        def proj_b():
            P.add("dve", I("tensor_scalar", out=svh, in0=gl, scalar1=rs, scalar2=None, op0=ALU.mult),
                  [("gl", k % 4), (gk, "rs")], [("svh", k % 4)])

        def mix():
            P.add("pe", I("matmul", psf(bm)[:, j * 128:(j + 1) * 128], lhsT=svh, rhs=sgw[:, g * 128:(g + 1) * 128],
                          start=True, stop=True, skip_group_check=True), [("svh", k % 4), "sgw"], [pskey(bm)])
            if j != len(tiles) - 1:
                return
            n = len(tiles)
            c0 = t4 * 128
            mt = mtmp[:, 0:n * 128]
            P.add("dve", I("scalar_tensor_tensor", out=mt.rearrange("p (j t) -> p j t", j=n),
                           in0=psf(bm)[:, 0:n * 128].rearrange("p (j t) -> p j t", j=n), scalar=sgn[:, g:g + 1],
                           in1=sgb[:, g * 128:(g + 1) * 128].unsqueeze(1).to_broadcast([128, n, 128]),
                           op0=ALU.mult, op1=ALU.add), [pskey(bm), "vec", "sgb"], ["mtmp"])
            P.add("dve", I("tensor_tensor", out=obT[:, g, c0:c0 + n * 128], in0=mt, in1=uT[:, c0:c0 + n * 128],
                           op=ALU.mult), ["mtmp"] + [("uT", g % 2, s_) for s_, _ in own_blocks], [("obT", g, t4)])

        return (proj, proj_b, mix)

    def _load_wus(g):
        dma_pool(wus_bufs[g % 2].rearrange("p c n -> p (c n)"), w_us[g], wus_sems[g % 2], writes=[("wus", g % 2)])

    _load_wus(0)

    def _u_proj(g, wus, wkey):
        def fn():
            if g + 1 < 8:
                _load_wus(g + 1)
            for (s0, w) in own_blocks:
                b = svbank[0] % 6; svbank[0] += 1
                pv = psf(b)[:, 0:w]
                hk = [("hT", t) for t in range(s0 // 128, (s0 + w) // 128)]
                for c in range(8):
                    P.add("pe", I("matmul", pv, lhsT=wus[:, c, 0:128], rhs=hT_own[:, c, s0:s0 + w],
                                  start=(c == 0), stop=(c == 7)), hk + [wkey], [pskey(b)])
                P.add("act", I("activation", out=uT_bufs[g % 2][:, s0:s0 + w], in_=pv, func=AF.Gelu),
                      [pskey(b)], [("uT", g % 2, s0)])
        return fn

    steps = []
    for g in range(8):
        wus = wus_bufs[g % 2]
        wkey = ("wus", g % 2)
        first = True
        for t4 in range(0, NTO, 4):
            tiles = list(range(t4, min(t4 + 4, NTO)))
            bm = 6 + (mixctr[0] % 2); mixctr[0] += 1
            for j, tt in enumerate(tiles):
                st = _sgu_tile(g, wus, wkey, t4, tiles, j, tt, bm)
                steps.append((st[0], st[1], st[2], _u_proj(g, wus, wkey) if first else None))
                first = False
    for n in range(len(steps) + 3):
        if n < len(steps):
            if steps[n][3] is not None:
                steps[n][3]()
            steps[n][0]()
        if 1 <= n <= len(steps):
            steps[n - 1][1]()
        if 3 <= n:
            steps[n - 3][2]()

    if "obT" in debug:
        dump("obT", ar.bf16(obT_off, 8 * TOWN))
    P.barrier()

    wout_pre = ar.bf16(TMP + 4 * TOWN, 8192)
    wout_sem = new_dsem("wout")
    dma_pool(wout_pre, w_out, wout_sem, writes=["wout"])
    b0 = B2_OFF
    wab_bufs = [ar.bf16(b0 + i * 2048, 4096).rearrange("p (c n) -> p c n", c=8) for i in range(2)]; b0 += 4096
    wab_sems = [new_dsem("wab%d" % i) for i in range(2)]
    sg_bufs = [ar.f32(b0 + i * 512, 512) for i in range(2)]; b0 += 1024
    t12 = [ar.f32(b0 + i * 512, 512) for i in range(2)]; b0 += 1024
    assert b0 <= NW, b0
    allh = [("hT", t) for t in range(NTO)]
    def _load_wab(fc):
        dma_pool(wab_bufs[fc % 2].rearrange("p c n -> p (c n)"), w_ab[fc], wab_sems[fc % 2], writes=[("wab", fc % 2)])

    _load_wab(0)
    for fc in range(8):
        wab = wab_bufs[fc % 2]
        wkey = ("wab", fc % 2)
        if fc + 1 < 8:
            _load_wab(fc + 1)
        for (s0, w) in own_blocks:
            banks = [next_bank() for _ in range(4)]
            srcs = [onT, obT, hT_own, hT_own]
            for q in range(4):
                pv = psf(banks[q])[:, 0:w]
                for c in range(8):
                    P.add("pe", I("matmul",
                        pv, lhsT=wab[:, c, q * 128:(q + 1) * 128], rhs=srcs[q][:, c, s0:s0 + w],
                        start=(c == 0), stop=(c == 7)),
                        [wkey, ("srcB2", q)], [pskey(banks[q])])
            for q in (2, 3):
                P.add("act", I("activation",
                    out=sg_bufs[q - 2][:, 0:w], in_=psf(banks[q])[:, 0:w], func=AF.Sigmoid),
                    [pskey(banks[q])], [("sg", q)])
            for q in (0, 1):
                P.add("dve", I("tensor_tensor",
                    out=t12[q][:, 0:w], in0=psf(banks[q])[:, 0:w], in1=sg_bufs[q][:, 0:w], op=ALU.mult),
                    [pskey(banks[q]), ("sg", q + 2)], [("t12", q)])
            P.add("pool", I("tensor_tensor",
                out=mT[:, fc, s0:s0 + w], in0=t12[0][:, 0:w], in1=t12[1][:, 0:w], op=ALU.add),
                [("t12", 0), ("t12", 1)], [("mT", fc, s0)])

    if "mT" in debug:
        dump("mT", ar.bf16(TMP, 8 * TOWN))

    P.barrier()
    xmid = ar.f32(R0, 16 * 1024).rearrange("p (t f) -> p t f", t=16)
    h2T = ar.bf16(R0 + 64 * KB, 8 * TOWN).rearrange("p (c t) -> p c t", c=8)
    c0 = TMP + 4 * TOWN
    wout = ar.bf16(c0, 8192).rearrange("p (c n) -> p c n", c=8); c0 += 4096
    xt2 = [ar.f32(c0 + i * 1024, 1024) for i in range(2)]; c0 += 2048
    xhalo = ar.f32(c0, 1024); c0 += 1024
    xn2 = [ar.bf16(c0 + i * 512, 1024) for i in range(2)]; c0 += 1024
    sqj2 = ar.bf16(c0, 1024); c0 += 512
    st2 = ar.f32(c0, 128); c0 += 128
    C_OFF = c0
    assert c0 <= NW
    xt2_sems = [new_dsem("xt2_%d" % i) for i in range(2)]
    def _b3_tile(tt):
        bi = tt % 2
        dst = xmid[:, tt, :] if tt < 16 else xhalo
        dkey = ("xmid", tt)
        sc = (tt % 32) * 4
        ss, ms, rs = st2[:, sc:sc + 1], st2[:, sc + 1:sc + 2], st2[:, sc + 2:sc + 3]
        xn = xn2[tt % 2]
        hold = {}

        def st_a():
            dma_sp(xt2[bi], xs[tt * 128:(tt + 1) * 128, :], xt2_sems[bi], writes=[("xt2", bi)])
            for nb in range(2):
                b = next_bank()
                for c in range(8):
                    P.add("pe", I("matmul", psf(b), lhsT=mT[:, c, tt * 128:(tt + 1) * 128],
                                  rhs=wout[:, c, nb * 512:(nb + 1) * 512], start=(c == 0), stop=(c == 7)),
                          ["wout", ("mTall",)], [pskey(b)])
                P.add("dve", I("tensor_tensor", out=dst[:, nb * 512:(nb + 1) * 512], in0=psf(b),
                               in1=xt2[bi][:, nb * 512:(nb + 1) * 512], op=ALU.add),
                      [pskey(b), ("xt2", bi)], [(dkey, nb)])
            P.add("act", I("activation", out=sqj2, in_=dst, func=AF.Square, accum_out=ss),
                  [(dkey, 0), (dkey, 1)], [("ss2", sc), "sqj2"])

        def st_b():
            P.add("dve", I("tensor_scalar", out=ms, in0=ss, scalar1=1.0 / D, scalar2=EPS, op0=ALU.mult,
                           op1=ALU.add), [("ss2", sc)], [("ms2", sc)])
            P.add("pool", I("tensor_tensor", out=rs, in0=ms, in1=negh, op=ALU.pow), [("ms2", sc), "vec"], [("rs2", sc)])

        def st_c():
            P.add("dve", I("tensor_scalar", out=xn, in0=dst, scalar1=rs, scalar2=None, op0=ALU.mult),
                  [(dkey, 0), (dkey, 1), ("rs2", sc)], [("xn2", tt % 2)])
            b = next_bank()
            hold["b"] = b
            pv = psb(b)
            for c in range(8):
                P.add("pe", I("transpose", out=pv[:, c * 128:(c + 1) * 128], in_=xn[:, c * 128:(c + 1) * 128],
                              identity=ident), [("xn2", tt % 2), "ident"], [pskey(b)])

        def st_d():
            b = hold["b"]
            pv = psb(b)
            P.add("dve", I("tensor_tensor", out=h2T[:, :, tt * 128:(tt + 1) * 128],
                           in0=pv.rearrange("p (c t) -> p c t", c=8),
                           in1=gffn.unsqueeze(2).to_broadcast([128, 8, 128]), op=ALU.mult),
                  [pskey(b), "vec"], [("h2T", tt)])

        return (st_a, st_b, st_c, st_d)

    b3 = [_b3_tile(tt) for tt in range(NTO)]
    for n in range(NTO + 3):
        for si in range(4):
            k_ = n - si
            if 0 <= k_ < NTO:
                b3[k_][si]()

    if "xmid" in debug or "h2T" in debug:
        P.barrier()
        if "xmid" in debug:
            s_ = new_dsem("dbg_xmid"); out_sems.append(s_)
            dma_sp(dbg["xmid"], xmid.rearrange("p t f -> p (t f)"), s_, reads=[])
        if "h2T" in debug:
            dump("h2T", ar.bf16(R0 + 64 * KB, 8 * TOWN))
        P.barrier()

    P.barrier()
    c0 = R0 + 98 * KB
    TB = 1024
    actT = ar.bf16(c0, NFC * TB).rearrange("p (j t) -> p j t", j=NFC); c0 += NFC * TB // 2
    wdn = ar.bf16(c0, NFC * 512).rearrange("p (j n) -> p j n", j=NFC); c0 += NFC * 256
    wup_bufs = [ar.bf16(c0 + i * 1024, 2048).rearrange("p (c n) -> p c n", c=8) for i in range(2)]; c0 += 2048
    cacc = [[ar.f32(c0 + (pp * 2 + i) * TB, TB) for i in range(2)] for pp in range(2)]; c0 += 4 * TB
    otile = [ar.f32(c0 + i * 512, 512) for i in range(2)]; c0 += 1024
    assert c0 <= NW, c0
    wup_sems = [new_dsem("wup%d" % i) for i in range(2)]
    wdn_sem = new_dsem("wdn")
    osem = [new_dsem("out%d" % i) for i in range(2)]
    out_sems.extend(osem)
    allh2 = [("h2T", t) for t in range(NTO)]
    uctr = [0]
    octr = [0]
    psflat = ps_all.rearrange("p b n -> p (b n)")
    for hf in range(2):
        a = hf * TB
        if hf == 0:
            segs = [(1, 0, 511), (512, 511, 512), (1024, 1023, 2)]
            for gv in range(2):
                P.add("dve", I("memset", psflat[:, gv * 1536:gv * 1536 + 1], 0.0), [], [pskey(3 * gv)])
        else:
            segs = [(0, a - 1, 512), (512, a + 511, 512), (1024, a + 1023, 2)]
        for j in range(NFC):
            k = uctr[0]; uctr[0] += 1
            wup = wup_bufs[k % 2]
            wkey = ("wup", k % 2)
            dma_pool(wup.rearrange("p c n -> p (c n)"), w_up[j], wup_sems[k % 2], writes=[wkey])
            accs = cacc[k % 2]
            for gv in range(2):
                base = gv * 1536
                banks = [3 * gv, 3 * gv + 1, 3 * gv + 2]
                for (dc, t0_, n) in segs:
                    b = 3 * gv + dc // 512
                    col = base + dc
                    for c in range(8):
                        P.add("pe", I("matmul", psflat[:, col:col + n], lhsT=wup[:, c, gv * 128:(gv + 1) * 128],
                                      rhs=h2T[:, c, t0_:t0_ + n], start=(c == 0), stop=(c == 7)),
                              [wkey] + allh2, [pskey(b)])
                cw = convp[:, (gv * NFC + j) * 4:(gv * NFC + j) * 4 + 4]
                acc = accs[gv]
                ak = ("cacc", k % 2, gv)
                rk = [pskey(b) for b in banks] + ["convp"]
                P.add("act", I("activation", out=acc, in_=psflat[:, base + 1:base + 1 + TB], func=AF.Identity,
                               bias=cw[:, 3:4], scale=cw[:, 1:2]), rk, [ak])
                P.add("dve", I("scalar_tensor_tensor", out=acc, in0=psflat[:, base:base + TB], scalar=cw[:, 0:1],
                               in1=acc, op0=ALU.mult, op1=ALU.add), rk + [ak], [ak])
                P.add("dve", I("scalar_tensor_tensor", out=acc, in0=psflat[:, base + 2:base + 2 + TB],
                               scalar=cw[:, 2:3], in1=acc, op0=ALU.mult, op1=ALU.add), rk + [ak], [ak])
            P.add("act", I("activation", out=accs[0], in_=accs[0], func=AF.Silu),
                  [("cacc", k % 2, 0)], [("cacc", k % 2, 0)])
            P.add("dve", I("tensor_tensor", out=actT[:, j, :], in0=accs[0], in1=accs[1], op=ALU.mult),
                  [("cacc", k % 2, 0), ("cacc", k % 2, 1)], [("actT", j)])
        for nb in range(2):
            dma_pool(wdn.rearrange("p j n -> p (j n)"), w_dn[nb], wdn_sem, writes=["wdn"])
            for tl in range(TB // 128):
                tt = hf * (TB // 128) + tl
                b = 6 + (octr[0] % 2)
                for j in range(NFC):
                    P.add("pe", I("matmul", psf(b), lhsT=actT[:, j, tl * 128:(tl + 1) * 128], rhs=wdn[:, j, :],
                                  start=(j == 0), stop=(j == NFC - 1)),
                          ["wdn"] + [("actT", jj) for jj in range(NFC)], [pskey(b)])
                oi = octr[0] % 2; octr[0] += 1
                ot = otile[oi]
                P.add("dve", I("tensor_tensor", out=ot, in0=psf(b), in1=xmid[:, tt, nb * 512:(nb + 1) * 512],
                               op=ALU.add), [pskey(b)], [("ot", oi)])
                dma_sp(out[tt * 128:(tt + 1) * 128, nb * 512:(nb + 1) * 512], ot, osem[oi], reads=[("ot", oi)])


def _host_consts():
    ident = np.eye(128, dtype=np.float32)
    blk = np.zeros((128, 128), np.float32)
    blk[:64, :64] = 1.0 / 64
    blk[64:, 64:] = 1.0 / 64
    p = np.arange(128, dtype=np.float64)[:, None]
    x = np.arange(896, dtype=np.float64)[None, :]
    Tm = (-np.abs(x - p - 384)).astype(np.float32)
    bias = np.zeros((128, H, NBL + NBR), np.float64)
    bqt = np.zeros((128, H, 8), np.float64)
    for h in range(H):
        sl = SLOPES[h]
        for j in range(NBL):
            bias[:, h, j] = sl * (p[:, 0] - 128.0 * j)
        for j in range(NBR):
            bias[:, h, NBL + j] = -sl * (p[:, 0] + 128.0 * j + 1.0)
        for i in range(4):
            bqt[:, h, i] = np.exp(-sl * (128.0 * i + p[:, 0]))
            bqt[:, h, 4 + i] = np.exp(-sl * (511.0 - 128.0 * i - p[:, 0]))
    return (ident, blk, Tm, bias.reshape(128, -1).astype(np.float32),
            bqt.reshape(128, -1).astype(np.float32))


def _pk(w):
    n = w.shape[1]
    return np.ascontiguousarray(w.reshape(8, 128, n).transpose(1, 0, 2).reshape(128, 8 * n))


def prepare_inputs(x, ln_mix_g, w_in, q_norm_g, k_norm_g, lambda_q1, lambda_k1, lambda_q2, lambda_k2,
                   subln_g, sg_norm_g, sg_w, sg_b, w_branch_a, w_branch_b, w_out, ln_ffn_g, w_up, conv_w,
                   conv_b, w_down):
    f = lambda a: np.asarray(a, dtype=np.float32)
    x = f(x); w_in = f(w_in)[0]; w_up = f(w_up)[0]; w_down = f(w_down)[0]
    ident, blk, Tm, bias, bqt = _host_consts()
    w_qkv = np.stack([_pk(np.concatenate([w_in[:, h * 128:(h + 1) * 128],
                                          w_in[:, 1024 + h * 128:1024 + (h + 1) * 128],
                                          w_in[:, 2048 + h * 128:2048 + (h + 1) * 128]], axis=1)) for h in range(H)])
    w_us = np.stack([_pk(np.concatenate([w_in[:, 3072 + g * 128:3072 + (g + 1) * 128],
                                         w_in[:, 4096 + g * 128:4096 + (g + 1) * 128]], axis=1)) for g in range(8)])
    wa = f(w_branch_a)[0]; wb = f(w_branch_b)[0]
    w_ab = np.stack([_pk(np.concatenate([wa[:, c * 128:(c + 1) * 128], wb[:, c * 128:(c + 1) * 128],
                                         w_in[:, 5120 + c * 128:5120 + (c + 1) * 128],
                                         w_in[:, 6144 + c * 128:6144 + (c + 1) * 128]], axis=1)) for c in range(8)])
    w_o = _pk(f(w_out)[0])
    w_upp = np.stack([_pk(np.concatenate([w_up[:, j * 128:(j + 1) * 128],
                                          w_up[:, DFF + j * 128:DFF + (j + 1) * 128]], axis=1)) for j in range(NFC)])
    wd = w_down.reshape(NFC, 128, 1024).transpose(1, 0, 2)
    w_dn = np.stack([np.ascontiguousarray(wd[:, :, nb * 512:(nb + 1) * 512]).reshape(128, NFC * 512) for nb in range(2)])
    vec = np.zeros((128, 64), np.float32)
    vec[:, 0:8] = f(ln_mix_g)[0].reshape(8, 128).T
    vec[:, 8] = np.concatenate([f(q_norm_g)[0]] * 2)
    vec[:, 9] = np.concatenate([f(k_norm_g)[0]] * 2)
    vec[:, 10] = f(subln_g)[0]
    vec[:, 11:19] = f(sg_norm_g)[0].reshape(8, 128).T
    vec[:, 19:27] = f(ln_ffn_g)[0].reshape(8, 128).T
    vec[:, 27] = -0.5
    lam = np.concatenate([f(lambda_q1)[0], f(lambda_k1)[0], f(lambda_q2)[0], f(lambda_k2)[0]])
    lamr = np.ascontiguousarray(np.broadcast_to(lam[None, :], (128, 256)))
    sgw0 = f(sg_w)[0]; sgb0 = f(sg_b)[0]; cw0 = f(conv_w)[0]; cb0 = f(conv_b)[0]
    per_parity = []
    for par in range(2):
        sgw_, sgb_, cw_ = sgw0, sgb0, cw0
        if par == 1:
            sgw_ = sgw0[:, ::-1, ::-1]
            sgb_ = sgb0[:, ::-1]
            cw_ = cw0[::-1, :]
        sgwT = np.ascontiguousarray(sgw_.transpose(2, 0, 1)).reshape(128, 1024)
        sgbr = np.ascontiguousarray(np.broadcast_to(sgb_.reshape(1, 1024), (128, 1024)))
        conv = np.zeros((128, 2, NFC, 4), np.float32)
        for gv in range(2):
            for j in range(NFC):
                cols = slice(gv * DFF + j * 128, gv * DFF + (j + 1) * 128)
                conv[:, gv, j, 0:3] = cw_[:, cols].T
                conv[:, gv, j, 3] = cb0[cols]
        per_parity.append((sgwT, sgbr, conv.reshape(128, -1)))
    in_maps = []
    for c in range(8):
        b, par = c // 2, c % 2
        xs_ = x[b] if par == 0 else x[b, ::-1]
        sgwT, sgbr, conv = per_parity[par]
        in_maps.append({
            "xs": np.ascontiguousarray(xs_), "c_ident": ident, "c_blk": blk, "c_T": Tm, "c_bias": bias,
            "c_bq": bqt, "c_vec": vec, "c_lam": lamr, "c_sgb": sgbr, "c_conv": conv,
            "w_qkv": w_qkv, "w_us": w_us, "w_ab": w_ab, "w_out": w_o, "w_up": w_upp, "w_dn": w_dn,
            "sg_wT": sgwT,
        })
    return in_maps


_NC_CACHE = {}


def kernel(**inputs):
    in_maps = prepare_inputs(**inputs)
    if "nc" not in _NC_CACHE:
        _NC_CACHE["nc"] = build()
    nc = _NC_CACHE["nc"]
    res = run_bass_kernel_spmd(nc, in_maps, core_ids=list(range(8)))
    outp = np.zeros((4, S, D), np.float32)
    for c in range(8):
        b, par = c // 2, c % 2
        o = np.asarray(res.results[c]["out"], dtype=np.float32)
        if par == 0:
            outp[b, 0:2048] = o
        else:
            outp[b, 2048:] = o[::-1]
    return outp
```

```python
import math
import numpy as np
import concourse.bass as bass
import concourse.mybir as mybir
from concourse.bass_utils import run_bass_kernel_spmd

F32 = mybir.dt.float32
BF16 = mybir.dt.bfloat16
AF = mybir.ActivationFunctionType
ALU = mybir.AluOpType

D = 1024
S = 4096
H = 8
DFF = 2816
NFC = DFF // 128
EPS = 1e-6
LAM_INIT = 0.8 - 0.6 * math.exp(0.0)
NT = 32
NTO = 17
TOWN = NTO * 128
TOTH = S - TOWN
VW = 130
SLOPES = [2.0 ** (-(h + 1)) for h in range(H)]
NBL = 18
NBR = 32
USE_V_FILLERS = False
SKIP_T = 192.0


class _Sem:
    def __init__(self, handle):
        self.handle = handle
        self.count = 0


class _Op:
    __slots__ = ("eng", "fn", "deps", "sem", "val", "signal", "is_dma", "idx")

    def __init__(self, eng, fn, is_dma):
        self.eng = eng
        self.fn = fn
        self.deps = []
        self.sem = None
        self.val = 0
        self.signal = False
        self.is_dma = is_dma


class Prog:
    ENGS = ("pe", "act", "dve", "pool", "sp")

    def __init__(self, esems):
        self.esems = esems
        self.ops = {e: [] for e in self.ENGS}
        self.last_w = {}
        self.readers = {}
        self.pending = {e: [] for e in self.ENGS}
        self.dsems = []

    def dsem(self, handle, group=False):
        s = _Sem(handle)
        s.group = group
        s.gops = []
        self.dsems.append(s)
        return s

    def add(self, eng, fn, reads=(), writes=(), dsem=None):
        op = _Op(eng, fn, dsem is not None)
        deps = []
        for k in reads:
            w = self.last_w.get(k)
            if w is not None:
                deps.append(w)
        for k in writes:
            w = self.last_w.get(k)
            if w is not None:
                deps.append(w)
            deps.extend(self.readers.get(k, ()))
        deps.extend(self.pending[eng])
        self.pending[eng] = []
        latest = {}
        seen = set()
        for d in deps:
            if d is op or id(d) in seen:
                continue
            seen.add(id(d))
            if d.eng == eng and eng == "pe" and not d.is_dma:
                continue
            if d.is_dma:
                op.deps.append(d)
            else:
                cur = latest.get(d.eng)
                if cur is None or d.idx > cur.idx:
                    latest[d.eng] = d
        op.deps.extend(latest.values())
        for k in reads:
            self.readers.setdefault(k, []).append(op)
        for k in writes:
            self.last_w[k] = op
            self.readers[k] = []
        if dsem is not None:
            dsem.count += 1
            op.sem = dsem
            op.val = 16 * dsem.count
            op.signal = True
            dsem.last = op
            dsem.gops.append(op)
        else:
            op.sem = self.esems[eng]
        op.idx = len(self.ops[eng])
        self.ops[eng].append(op)
        return op

    def barrier(self):
        lasts = []
        for e in self.ENGS:
            for op in reversed(self.ops[e]):
                if not op.is_dma:
                    lasts.append(op)
                    break
        for s in self.dsems:
            if getattr(s, "last", None) is not None:
                lasts.append(s.last)
        for e in self.ENGS:
            self.pending[e] = list(lasts)

    def emit(self, block, final_waits):
        for s in self.dsems:
            if s.group:
                for op in s.gops:
                    op.val = 16 * s.count
        for e in self.ENGS:
            for op in self.ops[e]:
                for d in op.deps:
                    d.signal = True
        for e in self.ENGS:
            c = 0
            for op in self.ops[e]:
                if not op.is_dma and op.signal:
                    c += 1
                    op.val = c

        def run(name, eng):
            waited = {}
            for op in self.ops[name]:
                need = {}
                for d in op.deps:
                    if need.get(d.sem, 0) < d.val:
                        need[d.sem] = d.val
                for s, v in need.items():
                    if waited.get(s, 0) < v:
                        eng.wait_ge(s.handle, v)
                        waited[s] = v
                ins = op.fn(eng)
                if op.is_dma:
                    ins.then_inc(op.sem.handle, 16)
                elif op.signal:
                    ins.then_inc(op.sem.handle, 1)
            if name == "sp":
                for s in final_waits:
                    eng.wait_ge(s.handle, 16 * s.count)

        @block.tensor
        def _(t):
            run("pe", t)

        @block.scalar
        def _(a):
            run("act", a)

        @block.vector
        def _(v):
            run("dve", v)

        @block.gpsimd
        def _(g):
            run("pool", g)

        @block.sync
        def _(s):
            run("sp", s)


class Arena:
    def __init__(self, t, nwords):
        self.t = t
        self.n = nwords

    def f32(self, off, n):
        assert off + n <= self.n, (off, n, self.n)
        return self.t[:, off:off + n]

    def bf16(self, off, n):
        w = (n + 1) // 2
        assert off + w <= self.n, (off, w, self.n)
        return self.t[:, off:off + w].bitcast(BF16)


def I(method, *args, **kwargs):
    def fn(e):
        return getattr(e, method)(*args, **kwargs)
    return fn


def _blocks(total, width=512):
    out = []
    s = 0
    while s < total:
        w = min(width, total - s)
        out.append((s, w))
        s += w
    return out


def build(debug=None, stop_after=None):
    debug = debug or ()
    nc = bass.Bass("TRN2", target_bir_lowering=False)

    def din(name, shape):
        return nc.dram_tensor(name, list(shape), F32, kind="ExternalInput").ap()

    xs = din("xs", [S, D])
    c_ident = din("c_ident", [128, 128])
    c_blk = din("c_blk", [128, 128])
    c_T = din("c_T", [128, 896])
    c_bias = din("c_bias", [128, H * (NBL + NBR)])
    c_bq = din("c_bq", [128, H * 8])
    c_vec = din("c_vec", [128, 64])
    c_lam = din("c_lam", [128, 256])
    c_sgb = din("c_sgb", [128, 1024])
    c_conv = din("c_conv", [128, 2 * NFC * 4])
    w_qkv = din("w_qkv", [H, 128, 8 * 384])
    w_us = din("w_us", [8, 128, 8 * 256])
    w_ab = din("w_ab", [8, 128, 8 * 512])
    w_out = din("w_out", [128, 8 * 1024])
    w_up = din("w_up", [NFC, 128, 8 * 256])
    w_dn = din("w_dn", [2, 128, NFC * 512])
    sg_wT = din("sg_wT", [128, 8 * 128])
    out = nc.dram_tensor("out", [2048, D], F32, kind="ExternalOutput").ap()
    dbg = {}
    dbg_shapes = {"hT": [128, 8 * TOWN], "onT": [128, 8 * TOWN], "obT": [128, 8 * TOWN],
                  "mT": [128, 8 * TOWN], "xmid": [128, 16 * 1024], "KT": [128, S],
                  "QT": [128, TOWN], "V": [128, NT * VW], "h2T": [128, 8 * TOWN]}
    for k in debug:
        dbg[k] = nc.dram_tensor("dbg_" + k, dbg_shapes[k], F32, kind="ExternalOutput").ap()

    NW = 53200
    import contextlib
    with contextlib.ExitStack() as es:
        arena_t = es.enter_context(nc.sbuf_tensor("arena", [128, NW], F32))
        ps_all = es.enter_context(nc.psum_tensor("ps_all", [128, 8, 512], F32))
        sem_names = ["pe", "act", "dve", "pool"]
        esems = {n: _Sem(es.enter_context(nc.semaphore("s_" + n))) for n in sem_names}
        esems["sp"] = _Sem(es.enter_context(nc.semaphore("s_sp")))
        P = Prog(esems)

        def new_dsem(name, group=False):
            return P.dsem(es.enter_context(nc.semaphore("d_" + name)), group=group)

        ar = Arena(arena_t, NW)

        o = 0
        def take(nwords):
            nonlocal o
            r = o
            o += nwords
            return r
        ident = ar.bf16(take(64), 128)
        blk = ar.bf16(take(64), 128)
        Tm = ar.f32(take(896), 896)
        biasT = ar.f32(take(H * (NBL + NBR)), H * (NBL + NBR))
        bq = ar.f32(take(H * 8), H * 8)
        vec = ar.f32(take(64), 64)
        lamv = ar.f32(take(256), 256)
        sgb = ar.f32(take(1024), 1024)
        convp = ar.f32(take(2 * NFC * 4), 2 * NFC * 4)
        sgw = ar.bf16(take(512), 1024)
        small = ar.f32(take(256), 256)
        CONST_END = o
        R0 = CONST_END
        KB = 256
        hT_own_off = R0
        onT_off = R0 + 34 * KB
        hT_oth_off = R0 + 68 * KB
        obT_off = hT_oth_off
        TMP = R0 + 102 * KB
        hT_own = ar.bf16(hT_own_off, 8 * TOWN).rearrange("p (c t) -> p c t", c=8)
        hT_oth = ar.bf16(hT_oth_off, 8 * TOTH).rearrange("p (c t) -> p c t", c=8)
        onT = ar.bf16(onT_off, 8 * TOWN).rearrange("p (c t) -> p c t", c=8)
        obT = ar.bf16(obT_off, 8 * TOWN).rearrange("p (c t) -> p c t", c=8)

        V_GMIX, V_GQ, V_GK, V_SUB, V_SGN, V_GFFN, V_NEGH = 0, 8, 9, 10, 11, 19, 27
        gmix = vec[:, V_GMIX:V_GMIX + 8]
        gffn = vec[:, V_GFFN:V_GFFN + 8]
        sgn = vec[:, V_SGN:V_SGN + 8]
        negh = vec[:, V_NEGH:V_NEGH + 1]
        gqs = small[:, 0:1]
        subs = small[:, 1:2]
        neglam = small[:, 2:3]
        d12 = small[:, 3:5]
        e12 = small[:, 5:7]
        ljunk = small[:, 8:72]

        dconst = new_dsem("const", group=True)
        dconst2 = new_dsem("const2", group=True)
        out_sems = []

        def dma_sp(out_ap, in_ap, sem, reads=(), writes=()):
            return P.add("sp", I("dma_start", out=out_ap, in_=in_ap), reads, writes, dsem=sem)

        def dma_pool(out_ap, in_ap, sem, reads=(), writes=()):
            return P.add("pool", I("dma_start", out=out_ap, in_=in_ap), reads, writes, dsem=sem)

        dma_pool(ident, c_ident, dconst2, writes=["ident"])
        dma_pool(blk, c_blk, dconst2, writes=["blk"])
        dma_pool(sgw, sg_wT, dconst2, writes=["sgw"])
        dma_sp(Tm, c_T, dconst, writes=["Tm"])
        dma_sp(biasT, c_bias, dconst, writes=["biasT"])
        dma_sp(bq, c_bq, dconst, writes=["bq"])
        dma_sp(vec, c_vec, dconst, writes=["vec"])
        dma_sp(lamv, c_lam, dconst, writes=["lamv"])
        dma_sp(sgb, c_sgb, dconst, writes=["sgb"])
        dma_sp(convp, c_conv, dconst, writes=["convp"])

        P.add("dve", I("tensor_scalar", out=gqs, in0=vec[:, V_GQ:V_GQ + 1], scalar1=0.125,
                                               scalar2=None, op0=ALU.mult), ["vec"], ["gqs"])
        P.add("dve", I("tensor_scalar", out=subs, in0=vec[:, V_SUB:V_SUB + 1],
                                               scalar1=float(1.0 - LAM_INIT), scalar2=None, op0=ALU.mult),
              ["vec"], ["subs"])
        P.add("dve", I("scalar_tensor_tensor", out=ljunk, in0=lamv[:, 0:64], scalar=1.0,
                                                      in1=lamv[:, 64:128], op0=ALU.mult, op1=ALU.mult,
                                                      accum_out=d12[:, 0:1]), ["lamv"], ["d1", "ljunk"])
        P.add("dve", I("scalar_tensor_tensor", out=ljunk, in0=lamv[:, 128:192], scalar=1.0,
                                                      in1=lamv[:, 192:256], op0=ALU.mult, op1=ALU.mult,
                                                      accum_out=d12[:, 1:2]), ["lamv"], ["d2", "ljunk"])
        P.add("act", I("activation", out=e12, in_=d12, func=AF.Exp), ["d1", "d2"], ["e12"])
        P.add("dve", I("tensor_tensor", out=neglam, in0=e12[:, 1:2], in1=e12[:, 0:1], op=ALU.subtract),
              ["e12"], ["nl0"])
        P.add("dve", I("tensor_scalar", out=neglam, in0=neglam, scalar1=float(-LAM_INIT), scalar2=None,
                                               op0=ALU.add), ["nl0"], ["neglam"])

        def psf(b):
            return ps_all[:, b, :]

        def psb(b):
            return ps_all[:, b, :].bitcast(BF16)

        bank_ctr = [0]

        def next_bank(lo=0, hi=8):
            b = lo + bank_ctr[0] % (hi - lo)
            bank_ctr[0] += 1
            return b

        def pskey(b):
            return ("ps", b)

        t0 = TMP
        NXT = 4
        xt_bufs = [ar.f32(t0 + i * 1024, 1024) for i in range(NXT)]
        t0 += NXT * 1024
        xn_bufs = [ar.bf16(t0 + i * 512, 1024) for i in range(2)]
        sqj = ar.bf16(t0 + 1024, 1024)
        stat = ar.f32(t0 + 1536, 128)
        PH_A = t0 + 1536 + 128
        xt_sems = [new_dsem("xt%d" % i) for i in range(NXT)]
        nctr = [0]

        def norm_tile(src_key, src_ap, gvec, dstT, col0, dst_key):
            i = nctr[0]
            nctr[0] += 1
            sc = (i % 32) * 4
            ss, ms, rs = stat[:, sc:sc + 1], stat[:, sc + 1:sc + 2], stat[:, sc + 2:sc + 3]
            xn = xn_bufs[i % 2]
            xnk = ("xn", i % 2)
            hold = {}

            def st_a():
                P.add("act", I("activation", out=sqj, in_=src_ap, func=AF.Square, accum_out=ss),
                      [src_key], [("ss", sc), "sqj"])

            def st_b():
                P.add("dve", I("tensor_scalar", out=ms, in0=ss, scalar1=1.0 / D, scalar2=EPS,
                               op0=ALU.mult, op1=ALU.add), [("ss", sc)], [("ms", sc)])
                P.add("pool", I("tensor_tensor", out=rs, in0=ms, in1=negh, op=ALU.pow),
                      [("ms", sc), "vec"], [("rs", sc)])

            def st_c():
                P.add("dve", I("tensor_scalar", out=xn, in0=src_ap, scalar1=rs, scalar2=None, op0=ALU.mult),
                      [src_key, ("rs", sc)], [xnk])
                b = next_bank()
                hold["b"] = b
                pv = psb(b)
                for c in range(8):
                    P.add("pe", I("transpose", out=pv[:, c * 128:(c + 1) * 128], in_=xn[:, c * 128:(c + 1) * 128],
                                  identity=ident), [xnk, "ident"], [pskey(b)])

            def st_d():
                pv = psb(hold["b"])
                P.add("dve", I("tensor_tensor", out=dstT[:, :, col0:col0 + 128],
                               in0=pv.rearrange("p (c t) -> p c t", c=8),
                               in1=gvec.unsqueeze(2).to_broadcast([128, 8, 128]), op=ALU.mult),
                      [pskey(hold["b"]), "vec"], [dst_key])

            return (st_a, st_b, st_c, st_d)

        if "onT" in debug:
            P.add("pool", I("memset", ar.bf16(onT_off, 8 * TOWN), 0.0), [], [("onT", hh, qq) for hh in range(H) for qq in range(NTO)])
        p0 = []
        for tt in range(NT):
            bi = tt % NXT
            xt = xt_bufs[bi]
            if tt < NTO:
                st = norm_tile(("xt", bi), xt, gmix, hT_own, tt * 128, ("hT", tt))
            else:
                st = norm_tile(("xt", bi), xt, gmix, hT_oth, (tt - NTO) * 128, ("hT", tt))
            p0.append((tt, bi, xt, st))
        for n in range(NT + 3):
            for si in range(4):
                k_ = n - si
                if 0 <= k_ < NT:
                    tt, bi, xt, st = p0[k_]
                    if si == 0:
                        dma_sp(xt, xs[tt * 128:(tt + 1) * 128, :], xt_sems[bi], writes=[("xt", bi)])
                    st[si]()

        def hT_block(tok0, w):
            if tok0 < TOWN:
                assert tok0 + w <= TOWN
                return hT_own, tok0, [("hT", t) for t in range(tok0 // 128, (tok0 + w + 127) // 128)]
            return hT_oth, tok0 - TOWN, [("hT", t) for t in range(tok0 // 128, (tok0 + w + 127) // 128)]

        own_blocks = _blocks(TOWN)
        all_blocks = own_blocks + [(TOWN + s, w) for s, w in _blocks(TOTH)]

        def dump(name, src_flat):
            P.barrier()
            s_ = new_dsem("dbg_" + name); out_sems.append(s_)
            dma_pool(dbg[name], src_flat, s_)
            P.barrier()

        if "hT" in debug:
            dump("hT", ar.bf16(hT_own_off, 8 * TOWN))

        a0 = PH_A
        KT = ar.bf16(a0, S); a0 += S // 2
        Va_flats = [ar.bf16(a0 + i * ((NT * VW) // 2), NT * VW) for i in range(2)]
        Va_bufs = [f_.rearrange("p (t v) -> p t v", v=VW) for f_ in Va_flats]
        Va_flat = Va_flats[0]
        a0 += NT * VW
        QT = ar.bf16(a0, TOWN); a0 += TOWN // 2
        wq_bufs = [ar.bf16(a0 + i * 1536, 3072).rearrange("p (c n) -> p c n", c=8) for i in range(2)]
        a0 += 3072
        wq_sems = [new_dsem("wq%d" % i) for i in range(2)]
        sq_bufs = [ar.bf16(a0 + i * 256, 512) for i in range(2)]; a0 += 512
        sd_bufs = [ar.f32(a0 + i * 512, 512) for i in range(2)]; a0 += 1024
        rs_bufs = [ar.f32(a0 + i * 512, 512) for i in range(2)]; a0 += 1024
        al = TMP
        NPB = 3
        Pb = [ar.bf16(al + i * 512, 1024).rearrange("p (m n) -> p m n", m=2) for i in range(NPB)]; al += NPB * 512
        sDb = [ar.f32(al + i * 1024, 1024).rearrange("p (m n) -> p m n", m=2) for i in range(2)]; al += 2048
        tots = [ar.f32(al + i * 8 * VW, 8 * VW).rearrange("p (i m v) -> p i m v", i=4, m=2) for i in range(2)]
        al += 16 * VW
        assert al <= PH_A
        osb = [ar.f32(a0 + i * 128, 128) for i in range(2)]; a0 += 256
        ojunk = ar.f32(a0, 128); a0 += 128
        o2b = [ar.f32(a0 + i * 128, 128) for i in range(2)]; a0 += 256
        onb = [ar.bf16(a0 + i * 64, 128) for i in range(4)]; a0 += 256
        est = ar.f32(a0, 64); a0 += 64
        Mh = ar.f32(a0, 896); a0 += 896
        assert a0 <= NW, a0

        for Va_ in Va_bufs:
            P.add("pool", I("memset", Va_[:, :, 128:129], 1.0), [], ["Vones"])
            P.add("pool", I("memset", Va_[:, :, 129:130], 0.0), [], ["Vpad"])

        pctr = [0]

        def qk_project(h, wq, wkey, col_lo, blocks, gvecap, gkey, dst, dst_key_fn):
            steps = []
            for (s0, w) in blocks:
                steps.append(_qk_block(wq, wkey, col_lo, s0, w, gvecap, gkey, dst, dst_key_fn))
            for n in range(len(steps) + 1):
                if n < len(steps):
                    steps[n][0]()
                if n >= 1:
                    steps[n - 1][1]()

        def _qk_block(wq, wkey, col_lo, s0, w, gvecap, gkey, dst, dst_key_fn):
            src, so, hkeys = hT_block(s0, w)
            i = pctr[0]; pctr[0] += 1
            bA = pbank[0] % 4; bB = 4 + (pbank[0] % 4); pbank[0] += 1
            pA = psf(bA)[:, 0:w]; pB = psf(bB)[:, 0:w]
            sq = sq_bufs[i % 2][:, 0:w]; sd = sd_bufs[i % 2][:, 0:w]

            def stage1():
                for c in range(8):
                    P.add("pe", I("matmul", pA, lhsT=wq[:, c, col_lo:col_lo + 128], rhs=src[:, c, so:so + w],
                                  start=(c == 0), stop=(c == 7)), hkeys + [wkey], [pskey(bA)])
                P.add("act", I("activation", out=sq, in_=pA, func=AF.Square), [pskey(bA)], [("sq", i % 2)])

            def stage2():
                P.add("pe", I("matmul", pB, lhsT=blk, rhs=sq, start=True, stop=True), [("sq", i % 2), "blk"], [pskey(bB)])
                rs = rs_bufs[i % 2][:, 0:w]
                P.add("act", I("activation", out=sd, in_=pB, func=AF.Ln, bias=EPS_AP, scale=1.0),
                      [pskey(bB), "epsap"], [("sd", i % 2)])
                P.add("act", I("activation", out=rs, in_=sd, func=AF.Exp, scale=-0.5),
                      [("sd", i % 2)], [("rsb", i % 2)])
                P.add("dve", I("scalar_tensor_tensor", out=dst[:, s0:s0 + w], in0=pA, scalar=gvecap, in1=rs,
                               op0=ALU.mult, op1=ALU.mult), [pskey(bA), ("rsb", i % 2), gkey], [dst_key_fn(s0)])

            return (stage1, stage2)

        pbank = [0]

        EPS_AP = small[:, 7:8]
        P.add("pool", I("memset", EPS_AP, EPS), [], ["epsap"])

        gctr = [0]
        dctr = [0]
        ectr = [0]
        tpctr = [0]
        psflat = ps_all.rearrange("p b n -> p (b n)")

        def acc_off(a):
            return (4 + a // 3) * 512 + (a % 3) * VW

        def acc_tile(a):
            o_ = acc_off(a)
            return psflat[:, o_:o_ + VW]

        def acc_pair(i):
            o0 = acc_off(2 * i)
            st = acc_off(2 * i + 1) - o0
            return psflat[:, o0:o0 + 2 * st].rearrange("p (m s) -> p m s", m=2)[:, :, 0:VW]

        def acc_bank(a):
            return 4 + a // 3

        blkctr = [0]
        pending_combine = []

        def make_jobs(h, q0, W):
            slope = SLOPES[h]
            nq = W // 128
            tb = blkctr[0] % 2; blkctr[0] += 1

            def far(kt):
                k_lo, k_hi = kt * 128, kt * 128 + 127
                dist = (q0 - k_hi) if k_hi < q0 else (k_lo - (q0 + W - 1))
                return slope * dist >= SKIP_T
            Lt = [kt for kt in range(NT) if (kt + 1) * 128 <= q0 and not far(kt)]
            Dt = [kt for kt in range(NT) if q0 <= kt * 128 < q0 + W]
            Rt = [kt for kt in range(NT) if kt * 128 >= q0 + W and not far(kt)]
            phases = [(ph, tl) for ph, tl in (("L", Lt), ("D", Dt), ("R", Rt)) if tl]
            jobs = []
            for pi, (phase, tiles) in enumerate(phases):
                for n, kt in enumerate(tiles):
                    buf = gctr[0] % 2; pbuf = gctr[0] % NPB; gctr[0] += 1
                    jobs.append(_make_job(h, q0, W, nq, slope, phase, pi == 0, pi == len(phases) - 1,
                                          n, len(tiles), kt, buf, tb, pbuf))
            return jobs

        def _make_job(h, q0, W, nq, slope, phase, first_phase, last_phase, n, ntiles, kt, buf, tb, pbuf):
            b0 = 2 * buf
            Skey = ("S", buf)
            Pk = ("P", pbuf)
            tot = tots[tb]

            def S_fn():
                P.add("pe", I("matmul", ps_all[:, b0, 0:W], lhsT=KT[0:64, kt * 128:(kt + 1) * 128],
                              rhs=QT[0:64, q0:q0 + W], start=True, stop=True, tile_position=(0, 0)),
                      ["KT", "QT"], [pskey(b0), Skey])
                P.add("pe", I("matmul", ps_all[:, b0 + 1, 0:W], lhsT=KT[64:128, kt * 128:(kt + 1) * 128],
                              rhs=QT[64:128, q0:q0 + W], start=True, stop=True, tile_position=(64, 0)),
                      ["KT", "QT"], [pskey(b0 + 1), Skey])

            def combine():
                for i in range(nq):
                    accv = acc_pair(i)
                    tv = tot[:, i, :, :]
                    tk = ("tot", tb, i)
                    if phase == "L":
                        sc = bq[:, h * 8 + i: h * 8 + i + 1]
                    elif phase == "R":
                        ii = i if W == 512 else 3
                        sc = bq[:, h * 8 + 4 + ii: h * 8 + 4 + ii + 1]
                    else:
                        sc = None
                    rk = [pskey(acc_bank(2 * i)), pskey(acc_bank(2 * i + 1)), ("acc", 2 * i), ("acc", 2 * i + 1), "bq"]
                    if first_phase:
                        if sc is None:
                            P.add("dve", I("tensor_copy", out=tv, in_=accv), rk, [tk])
                        else:
                            P.add("dve", I("tensor_scalar", out=tv, in0=accv, scalar1=sc, scalar2=None,
                                           op0=ALU.mult), rk, [tk])
                    else:
                        if sc is None:
                            P.add("dve", I("tensor_tensor", out=tv, in0=accv, in1=tv, op=ALU.add), rk + [tk], [tk])
                        else:
                            P.add("dve", I("scalar_tensor_tensor", out=tv, in0=accv, scalar=sc, in1=tv,
                                           op0=ALU.mult, op1=ALU.add), rk + [tk], [tk])

            def act_fn():
                Sin = ps_all[:, b0:b0 + 2, 0:W]
                pout = Pb[pbuf][:, :, 0:W]
                if phase == "D":
                    jD = kt - q0 // 128
                    db = dctr[0] % 2; dctr[0] += 1
                    sd_ = sDb[db][:, :, 0:W]
                    msl = Mh[:, 384 - 128 * jD: 384 - 128 * jD + W]
                    P.add("act", I("activation", out=sd_, in_=Sin, func=AF.Exp),
                          [Skey, pskey(b0), pskey(b0 + 1)], [("sD", db)])
                    P.add("dve", I("tensor_tensor", out=pout, in0=sd_,
                                   in1=msl.unsqueeze(1).to_broadcast([128, 2, W]), op=ALU.mult),
                          [("sD", db), "Mh"], [(Pk, 0), (Pk, 1)])
                else:
                    if phase == "L":
                        j = (q0 - kt * 128) // 128
                        col = h * (NBL + NBR) + j
                    else:
                        j = (kt * 128 - q0 - W) // 128
                        col = h * (NBL + NBR) + NBL + j
                    bap = biasT[:, col:col + 1]
                    P.add("act", I("activation", out=pout, in_=Sin, func=AF.Exp, bias=bap, scale=1.0),
                          [Skey, pskey(b0), pskey(b0 + 1), "biasT"], [(Pk, 0), (Pk, 1)])
            def av_fn():
                while pending_combine:
                    pending_combine.pop(0)()
                for i in range(nq):
                    for m in range(2):
                        a = 2 * i + m
                        P.add("pe", I("matmul", acc_tile(a), lhsT=Pb[pbuf][:, m, i * 128:(i + 1) * 128],
                                      rhs=Va_bufs[h % 2][:, kt, :], start=(n == 0 and a % 3 == 0),
                                      stop=(n == ntiles - 1), skip_group_check=True),
                              [(Pk, m), ("V", h % 2), "Vones", "Vpad"], [pskey(acc_bank(a)), ("acc", a)])
                if n == ntiles - 1:
                    pending_combine.append(combine)

            deferred = []
            if n == ntiles - 1 and last_phase:
                for i in range(nq):
                    k = ectr[0]; ectr[0] += 1
                    qt = q0 // 128 + i
                    deferred.append((2 + i, _epi_a(tot, tb, i, k), "a"))
                    deferred.append((4 + i, _epi_b(k), "b"))
                    deferred.append((6 + i, _epi_tail(h, k, qt), "t"))
            is_last_D = (phase == "D" and n == ntiles - 1)
            return (S_fn, act_fn, av_fn, deferred, is_last_D)

        def _epi_slots(k):
            eb = (k % 8) * 8
            return (est[:, eb:eb + 2], est[:, eb + 2:eb + 3], est[:, eb + 3:eb + 4], est[:, eb + 4:eb + 5],
                    est[:, eb + 5:eb + 6], ("est", k % 8))

        def _epi_a(tot, tb, i, k):
            rz, nl2, ss, ms, rs, ek = _epi_slots(k)
            o1 = osb[0]; o2 = o2b[k % 2]
            tk = ("tot", tb, i)

            def fn():
                P.add("dve", I("reciprocal", out=rz, in_=tot[:, i, :, 128]), [tk], [ek])
                P.add("dve", I("tensor_scalar", out=nl2, in0=rz[:, 1:2], scalar1=neglam, scalar2=None,
                               op0=ALU.mult), [ek, "neglam"], [(ek, "nl2")])
                P.add("dve", I("tensor_scalar", out=o1, in0=tot[:, i, 0, 0:128], scalar1=rz[:, 0:1],
                               scalar2=None, op0=ALU.mult), [tk, ek], ["o1"])
                P.add("dve", I("scalar_tensor_tensor", out=o2, in0=tot[:, i, 1, 0:128], scalar=nl2, in1=o1,
                               op0=ALU.mult, op1=ALU.add), [tk, (ek, "nl2"), "o1"], [("o2", k % 2)])
                P.add("dve", I("scalar_tensor_tensor", out=ojunk, in0=o2, scalar=1.0, in1=o2, op0=ALU.mult,
                               op1=ALU.mult, accum_out=ss), [("o2", k % 2)], [(ek, "ss"), "ojunk"])
                P.add("dve", I("tensor_scalar", out=ms, in0=ss, scalar1=1.0 / 128, scalar2=EPS, op0=ALU.mult,
                               op1=ALU.add), [(ek, "ss")], [(ek, "ms")])
                P.add("pool", I("tensor_tensor", out=rs, in0=ms, in1=negh, op=ALU.pow),
                      [(ek, "ms"), "vec"], [(ek, "rs")])
            return fn

        def _epi_b(k):
            rz, nl2, ss, ms, rs, ek = _epi_slots(k)

            def fn():
                P.add("dve", I("tensor_scalar", out=onb[k % 4], in0=o2b[k % 2], scalar1=rs, scalar2=None,
                               op0=ALU.mult), [("o2", k % 2), (ek, "rs")], [("on", k % 4)])
            return fn

        def _epi_tail(h, k, qt):
            def fn():
                sl = tpctr[0] % 8; tpctr[0] += 1
                pv = psb(7)[:, sl * 128:(sl + 1) * 128]
                P.add("pe", I("transpose", out=pv, in_=onb[k % 4], identity=ident),
                      [("on", k % 4), "ident"], [pskey(7)])
                P.add("dve", I("tensor_scalar", out=onT[:, h, qt * 128:(qt + 1) * 128], in0=pv, scalar1=subs,
                               scalar2=None, op0=ALU.mult), [pskey(7), "subs"], [("onT", h, qt)])
            return fn

        def run_pipeline(jobs, fillers=(), fill_pos=()):
            pend = []
            fillers = list(fillers)
            nfill = [0]
            N_ = len(jobs)
            for n in range(N_ + 2):
                if n >= 2:
                    jobs[n - 2][1]()
                if n < N_:
                    jobs[n][0]()
                if n >= 2:
                    jobs[n - 2][2]()
                    if jobs[n - 2][4]:
                        for it in pend:
                            it[2] = True
                    for (d, fn, kind) in jobs[n - 2][3]:
                        pend.append([d, fn, False, kind])
                    if (n - 2) in fill_pos and fillers:
                        fillers.pop(0)()
                        nfill[0] += 1
                group_open = (nfill[0] % 4) != 0
                keep = []
                for it in pend:
                    if it[2]:
                        it[0] -= 1
                    if it[2] and it[0] <= 0 and not (it[3] == "t" and group_open):
                        it[1]()
                    else:
                        keep.append(it)
                pend = keep
            while pending_combine:
                pending_combine.pop(0)()
            while (nfill[0] % 4) != 0 and fillers:
                fillers.pop(0)()
                nfill[0] += 1
            pend.sort(key=lambda it: it[0])
            for it in pend:
                it[1]()
            for f_ in fillers:
                f_()

        def v_group(h, g4, bank, eng):
            wq = wq_bufs[h % 2]
            wkey = ("wq", h % 2)
            Va_ = Va_bufs[h % 2]

            def part(j):
                def fn():
                    tt = g4 * 4 + j
                    src, so, hkeys = hT_block(tt * 128, 128)
                    for c in range(8):
                        P.add("pe", I("matmul", psf(bank)[:, j * 128:(j + 1) * 128], lhsT=src[:, c, so:so + 128],
                                      rhs=wq[:, c, 256:384], start=(c == 0), stop=(c == 7), skip_group_check=True),
                              hkeys + [wkey], [pskey(bank)])
                    if j != 3:
                        return
                    src_v = psf(bank).rearrange("p (j d) -> p j d", j=4)
                    dst_v = Va_[:, g4 * 4:(g4 + 1) * 4, 0:128]
                    if eng == "act":
                        P.add("act", I("activation", out=dst_v, in_=src_v, func=AF.Copy), [pskey(bank)], [("V", h % 2)])
                    else:
                        P.add("dve", I("tensor_copy", out=dst_v, in_=src_v), [pskey(bank)], [("V", h % 2)])
                return fn
            return [part(j) for j in range(4)]

        def load_wq(h):
            dma_pool(wq_bufs[h % 2].rearrange("p c n -> p (c n)"), w_qkv[h], wq_sems[h % 2], writes=[("wq", h % 2)])

        load_wq(0)
        for h in range(H):
            wq = wq_bufs[h % 2]
            wkey = ("wq", h % 2)
            qk_project(h, wq, wkey, 128, all_blocks, vec[:, V_GK:V_GK + 1], "vec", KT, lambda s0: "KT")
            qk_project(h, wq, wkey, 0, own_blocks, gqs, "gqs", QT, lambda s0: "QT")
            if h == 0 or not USE_V_FILLERS:
                for g4 in range(NT // 4):
                    for f_ in v_group(h, g4, next_bank(), "act"):
                        f_()
            if h + 1 < H:
                load_wq(h + 1)
            if h == 0:
                for nm, src_ap in (("KT", KT), ("QT", QT), ("V", Va_flat)):
                    if nm in debug:
                        dump(nm, src_ap)
            if stop_after == "proj0":
                break
            P.add("act", I("activation", out=Mh, in_=Tm, func=AF.Exp, scale=float(SLOPES[h])), ["Tm"], ["Mh"])
            jobs = []
            fill_pos = []
            for (q0, W) in own_blocks:
                jb = make_jobs(h, q0, W)
                base = len(jobs)
                if len(jb) >= 22:
                    fill_pos += [base + 10 + i_ for i_ in range(4)] + [base + 16 + i_ for i_ in range(4)]
                elif len(jb) >= 15:
                    fill_pos += [base + 10 + i_ for i_ in range(4)]
                jobs.extend(jb)
            fillers = []
            if USE_V_FILLERS and h + 1 < H and stop_after != "head0":
                fillers = [f_ for g4 in range(NT // 4) for f_ in v_group(h + 1, g4, 7, "dve")]
            run_pipeline(jobs, fillers, set(fill_pos))
            if stop_after == "head0":
                break

        if "onT" in debug:
            dump("onT", ar.bf16(onT_off, 8 * TOWN))

        done = stop_after in ("proj0", "head0", "attn")
        if not done:
            build_rest(nc, P, ar, ps_all, locals())

        with nc.Block() as block:
            P.emit(block, out_sems)
    return nc


def build_rest(nc, P, ar, ps_all, L):
    names = ("TMP hT_own onT obT sgw sgb sgn vec negh ident gffn convp xs w_us w_ab w_out w_up w_dn out debug dbg "
             "new_dsem dma_sp dma_pool psf psb pskey next_bank own_blocks small R0 KB out_sems dump obT_off").split()
    (TMP, hT_own, onT, obT, sgw, sgb, sgn, vec, negh, ident, gffn, convp, xs, w_us, w_ab, w_out, w_up, w_dn, out,
     debug, dbg, new_dsem, dma_sp, dma_pool, psf, psb, pskey, next_bank, own_blocks, small, R0, KB, out_sems,
     dump, obT_off) = [L[n] for n in names]
    NW = ar.n
    stop_after = L["stop_after"]

    P.barrier()
    b0 = TMP
    mT = ar.bf16(b0, 8 * TOWN).rearrange("p (c t) -> p c t", c=8); b0 += 4 * TOWN
    uT_bufs = [ar.f32(b0 + i * TOWN, TOWN) for i in range(2)]; b0 += 2 * TOWN
    wus_bufs = [ar.bf16(b0 + i * 1024, 2048).rearrange("p (c n) -> p c n", c=8) for i in range(2)]; b0 += 2048
    wus_sems = [new_dsem("wus%d" % i) for i in range(2)]
    gl_bufs = [ar.f32(b0 + i * 128, 128) for i in range(4)]; b0 += 512
    gjunk = ar.f32(b0, 128); b0 += 128
    svh_bufs = [ar.bf16(b0 + i * 64, 128) for i in range(4)]; b0 += 256
    mtmp = ar.f32(b0, 512); b0 += 512
    gst = ar.f32(b0, 64); b0 += 64
    B2_OFF = b0
    wab_bufs = [ar.bf16(B2_OFF + i * 2048, 4096).rearrange("p (c n) -> p c n", c=8) for i in range(2)]
    wab_sems = [new_dsem("wab%d" % i) for i in range(2)]

    def _load_wab(fc):
        dma_pool(wab_bufs[fc % 2].rearrange("p c n -> p (c n)"), w_ab[fc], wab_sems[fc % 2], writes=[("wab", fc % 2)])

    _load_wab(0)
    sctr = [0]
    mixctr = [0]
    svbank = [0]

    def _sgu_tile(g, wus, wkey, t4, tiles, j, tt, bm):
        uT = uT_bufs[g % 2]
        k = sctr[0]; sctr[0] += 1
        b = svbank[0] % 6; svbank[0] += 1
        pv = psf(b)[:, 0:128]
        gl = gl_bufs[k % 4]; svh = svh_bufs[k % 4]
        sb_ = (k % 8) * 4
        ss, ms, rs = gst[:, sb_:sb_ + 1], gst[:, sb_ + 1:sb_ + 2], gst[:, sb_ + 2:sb_ + 3]
        gk = ("gst", k % 8)

        def proj():
            for c in range(8):
                P.add("pe", I("matmul", pv, lhsT=hT_own[:, c, tt * 128:(tt + 1) * 128], rhs=wus[:, c, 128:256],
                              start=(c == 0), stop=(c == 7)), [("hT", tt), wkey], [pskey(b)])
            P.add("act", I("activation", out=gl, in_=pv, func=AF.Gelu), [pskey(b)], [("gl", k % 4)])
            P.add("dve", I("scalar_tensor_tensor", out=gjunk, in0=gl, scalar=1.0, in1=gl, op0=ALU.mult,
                           op1=ALU.mult, accum_out=ss), [("gl", k % 4)], [(gk, "ss"), "gjunk"])
            P.add("dve", I("tensor_scalar", out=ms, in0=ss, scalar1=1.0 / 128, scalar2=EPS, op0=ALU.mult,
                           op1=ALU.add), [(gk, "ss")], [(gk, "ms")])
            P.add("pool", I("tensor_tensor", out=rs, in0=ms, in1=negh, op=ALU.pow), [(gk, "ms"), "vec"], [(gk, "rs")])

        def proj_b():
            P.add("dve", I("tensor_scalar", out=svh, in0=gl, scalar1=rs, scalar2=None, op0=ALU.mult),
                  [("gl", k % 4), (gk, "rs")], [("svh", k % 4)])

        def mix():
            P.add("pe", I("matmul", psf(bm)[:, j * 128:(j + 1) * 128], lhsT=svh, rhs=sgw[:, g * 128:(g + 1) * 128],
                          start=True, stop=True, skip_group_check=True), [("svh", k % 4), "sgw"], [pskey(bm)])
            if j != len(tiles) - 1:
                return
            n = len(tiles)
            c0 = t4 * 128
            mt = mtmp[:, 0:n * 128]
            P.add("dve", I("scalar_tensor_tensor", out=mt.rearrange("p (j t) -> p j t", j=n),
                           in0=psf(bm)[:, 0:n * 128].rearrange("p (j t) -> p j t", j=n), scalar=sgn[:, g:g + 1],
                           in1=sgb[:, g * 128:(g + 1) * 128].unsqueeze(1).to_broadcast([128, n, 128]),
                           op0=ALU.mult, op1=ALU.add), [pskey(bm), "vec", "sgb"], ["mtmp"])
            P.add("dve", I("tensor_tensor", out=obT[:, g, c0:c0 + n * 128], in0=mt, in1=uT[:, c0:c0 + n * 128],
                           op=ALU.mult), ["mtmp"] + [("uT", g % 2, s_) for s_, _ in own_blocks], [("obT", g, t4)])

        return (proj, proj_b, mix)

    def _load_wus(g):
        dma_pool(wus_bufs[g % 2].rearrange("p c n -> p (c n)"), w_us[g], wus_sems[g % 2], writes=[("wus", g % 2)])

    _load_wus(0)

    def _u_proj(g, wus, wkey):
        def fn():
            if g + 1 < 8:
                _load_wus(g + 1)
            for (s0, w) in own_blocks:
                b = svbank[0] % 6; svbank[0] += 1
                pv = psf(b)[:, 0:w]
                hk = [("hT", t) for t in range(s0 // 128, (s0 + w) // 128)]
                for c in range(8):
                    P.add("pe", I("matmul", pv, lhsT=wus[:, c, 0:128], rhs=hT_own[:, c, s0:s0 + w],
                                  start=(c == 0), stop=(c == 7)), hk + [wkey], [pskey(b)])
                P.add("act", I("activation", out=uT_bufs[g % 2][:, s0:s0 + w], in_=pv, func=AF.Gelu),
                      [pskey(b)], [("uT", g % 2, s0)])
        return fn

    steps = []
    for g in range(8):
        wus = wus_bufs[g % 2]
        wkey = ("wus", g % 2)
        first = True
        for t4 in range(0, NTO, 4):
            tiles = list(range(t4, min(t4 + 4, NTO)))
            bm = 6 + (mixctr[0] % 2); mixctr[0] += 1
            for j, tt in enumerate(tiles):
                st = _sgu_tile(g, wus, wkey, t4, tiles, j, tt, bm)
                steps.append((st[0], st[1], st[2], _u_proj(g, wus, wkey) if first else None))
                first = False
    for n in range(len(steps) + 3):
        if n < len(steps):
            if steps[n][3] is not None:
                steps[n][3]()
            steps[n][0]()
        if 1 <= n <= len(steps):
            steps[n - 1][1]()
        if 3 <= n:
            steps[n - 3][2]()

    if "obT" in debug:
        dump("obT", ar.bf16(obT_off, 8 * TOWN))
    P.barrier()

    wout_pre = ar.bf16(TMP + 4 * TOWN, 8192)
    wout_sem = new_dsem("wout")
    dma_pool(wout_pre, w_out, wout_sem, writes=["wout"])
    b0 = B2_OFF + 4096
    sg_bufs = [ar.f32(b0 + i * 512, 512) for i in range(2)]; b0 += 1024
    t12 = [ar.f32(b0 + i * 512, 512) for i in range(2)]; b0 += 1024
    assert b0 <= NW, b0
    allh = [("hT", t) for t in range(NTO)]
    for fc in range(8):
        wab = wab_bufs[fc % 2]
        wkey = ("wab", fc % 2)
        if fc + 1 < 8:
            _load_wab(fc + 1)
        for (s0, w) in own_blocks:
            banks = [next_bank() for _ in range(4)]
            srcs = [onT, obT, hT_own, hT_own]
            for q in range(4):
                pv = psf(banks[q])[:, 0:w]
                for c in range(8):
                    P.add("pe", I("matmul",
                        pv, lhsT=wab[:, c, q * 128:(q + 1) * 128], rhs=srcs[q][:, c, s0:s0 + w],
                        start=(c == 0), stop=(c == 7)),
                        [wkey, ("srcB2", q)], [pskey(banks[q])])
            for q in (2, 3):
                P.add("act", I("activation",
                    out=sg_bufs[q - 2][:, 0:w], in_=psf(banks[q])[:, 0:w], func=AF.Sigmoid),
                    [pskey(banks[q])], [("sg", q)])
            for q in (0, 1):
                P.add("dve", I("tensor_tensor",
                    out=t12[q][:, 0:w], in0=psf(banks[q])[:, 0:w], in1=sg_bufs[q][:, 0:w], op=ALU.mult),
                    [pskey(banks[q]), ("sg", q + 2)], [("t12", q)])
            P.add("pool", I("tensor_tensor",
                out=mT[:, fc, s0:s0 + w], in0=t12[0][:, 0:w], in1=t12[1][:, 0:w], op=ALU.add),
                [("t12", 0), ("t12", 1)], [("mT", fc, s0)])

    if "mT" in debug:
        dump("mT", ar.bf16(TMP, 8 * TOWN))

    P.barrier()
    xmid = ar.f32(R0, 16 * 1024).rearrange("p (t f) -> p t f", t=16)
    h2T = ar.bf16(R0 + 64 * KB, 8 * TOWN).rearrange("p (c t) -> p c t", c=8)
    c0 = TMP + 4 * TOWN
    wout = ar.bf16(c0, 8192).rearrange("p (c n) -> p c n", c=8); c0 += 4096
    xt2 = [ar.f32(c0 + i * 1024, 1024) for i in range(2)]; c0 += 2048
    xhalo = ar.f32(c0, 1024); c0 += 1024
    xn2 = [ar.bf16(c0 + i * 512, 1024) for i in range(2)]; c0 += 1024
    sqj2 = ar.bf16(c0, 1024); c0 += 512
    st2 = ar.f32(c0, 128); c0 += 128
    C_OFF = c0
    assert c0 <= NW
    xt2_sems = [new_dsem("xt2_%d" % i) for i in range(2)]
    def _b3_tile(tt):
        bi = tt % 2
        dst = xmid[:, tt, :] if tt < 16 else xhalo
        dkey = ("xmid", tt)
        sc = (tt % 32) * 4
        ss, ms, rs = st2[:, sc:sc + 1], st2[:, sc + 1:sc + 2], st2[:, sc + 2:sc + 3]
        xn = xn2[tt % 2]
        hold = {}

        def st_a():
            dma_sp(xt2[bi], xs[tt * 128:(tt + 1) * 128, :], xt2_sems[bi], writes=[("xt2", bi)])
            for nb in range(2):
                b = next_bank()
                for c in range(8):
                    P.add("pe", I("matmul", psf(b), lhsT=mT[:, c, tt * 128:(tt + 1) * 128],
                                  rhs=wout[:, c, nb * 512:(nb + 1) * 512], start=(c == 0), stop=(c == 7)),
                          ["wout", ("mTall",)], [pskey(b)])
                P.add("dve", I("tensor_tensor", out=dst[:, nb * 512:(nb + 1) * 512], in0=psf(b),
                               in1=xt2[bi][:, nb * 512:(nb + 1) * 512], op=ALU.add),
                      [pskey(b), ("xt2", bi)], [(dkey, nb)])
            P.add("act", I("activation", out=sqj2, in_=dst, func=AF.Square, accum_out=ss),
                  [(dkey, 0), (dkey, 1)], [("ss2", sc), "sqj2"])

        def st_b():
            P.add("dve", I("tensor_scalar", out=ms, in0=ss, scalar1=1.0 / D, scalar2=EPS, op0=ALU.mult,
                           op1=ALU.add), [("ss2", sc)], [("ms2", sc)])
            P.add("pool", I("tensor_tensor", out=rs, in0=ms, in1=negh, op=ALU.pow), [("ms2", sc), "vec"], [("rs2", sc)])

        def st_c():
            P.add("dve", I("tensor_scalar", out=xn, in0=dst, scalar1=rs, scalar2=None, op0=ALU.mult),
                  [(dkey, 0), (dkey, 1), ("rs2", sc)], [("xn2", tt % 2)])
            b = next_bank()
            hold["b"] = b
            pv = psb(b)
            for c in range(8):
                P.add("pe", I("transpose", out=pv[:, c * 128:(c + 1) * 128], in_=xn[:, c * 128:(c + 1) * 128],
                              identity=ident), [("xn2", tt % 2), "ident"], [pskey(b)])

        def st_d():
            b = hold["b"]
            pv = psb(b)
            P.add("dve", I("tensor_tensor", out=h2T[:, :, tt * 128:(tt + 1) * 128],
                           in0=pv.rearrange("p (c t) -> p c t", c=8),
                           in1=gffn.unsqueeze(2).to_broadcast([128, 8, 128]), op=ALU.mult),
                  [pskey(b), "vec"], [("h2T", tt)])

        return (st_a, st_b, st_c, st_d)

    b3 = [_b3_tile(tt) for tt in range(NTO)]
    for n in range(NTO + 3):
        for si in range(4):
            k_ = n - si
            if 0 <= k_ < NTO:
                b3[k_][si]()

    if "xmid" in debug or "h2T" in debug:
        P.barrier()
        if "xmid" in debug:
            s_ = new_dsem("dbg_xmid"); out_sems.append(s_)
            dma_sp(dbg["xmid"], xmid.rearrange("p t f -> p (t f)"), s_, reads=[])
        if "h2T" in debug:
            dump("h2T", ar.bf16(R0 + 64 * KB, 8 * TOWN))
        P.barrier()

    P.barrier()
    c0 = R0 + 98 * KB
    TB = 1024
    actT = ar.bf16(c0, NFC * TB).rearrange("p (j t) -> p j t", j=NFC); c0 += NFC * TB // 2
    wdn = ar.bf16(c0, NFC * 512).rearrange("p (j n) -> p j n", j=NFC); c0 += NFC * 256
    wup_bufs = [ar.bf16(c0 + i * 1024, 2048).rearrange("p (c n) -> p c n", c=8) for i in range(2)]; c0 += 2048
    cacc = [[ar.f32(c0 + (pp * 2 + i) * TB, TB) for i in range(2)] for pp in range(2)]; c0 += 4 * TB
    otile = [ar.f32(c0 + i * 512, 512) for i in range(2)]; c0 += 1024
    assert c0 <= NW, c0
    wup_sems = [new_dsem("wup%d" % i) for i in range(2)]
    wdn_sem = new_dsem("wdn")
    osem = [new_dsem("out%d" % i) for i in range(2)]
    out_sems.extend(osem)
    allh2 = [("h2T", t) for t in range(NTO)]
    uctr = [0]
    octr = [0]
    psflat = ps_all.rearrange("p b n -> p (b n)")
    for hf in range(2):
        a = hf * TB
        if hf == 0:
            segs = [(1, 0, 511), (512, 511, 512), (1024, 1023, 2)]
            for gv in range(2):
                P.add("dve", I("memset", psflat[:, gv * 1536:gv * 1536 + 1], 0.0), [], [pskey(3 * gv)])
        else:
            segs = [(0, a - 1, 512), (512, a + 511, 512), (1024, a + 1023, 2)]
        for j in range(NFC):
            k = uctr[0]; uctr[0] += 1
            wup = wup_bufs[k % 2]
            wkey = ("wup", k % 2)
            dma_pool(wup.rearrange("p c n -> p (c n)"), w_up[j], wup_sems[k % 2], writes=[wkey])
            accs = cacc[k % 2]
            for gv in range(2):
                base = gv * 1536
                banks = [3 * gv, 3 * gv + 1, 3 * gv + 2]
                for (dc, t0_, n) in segs:
                    b = 3 * gv + dc // 512
                    col = base + dc
                    for c in range(8):
                        P.add("pe", I("matmul", psflat[:, col:col + n], lhsT=wup[:, c, gv * 128:(gv + 1) * 128],
                                      rhs=h2T[:, c, t0_:t0_ + n], start=(c == 0), stop=(c == 7)),
                              [wkey] + allh2, [pskey(b)])
                cw = convp[:, (gv * NFC + j) * 4:(gv * NFC + j) * 4 + 4]
                acc = accs[gv]
                ak = ("cacc", k % 2, gv)
                rk = [pskey(b) for b in banks] + ["convp"]
                P.add("act", I("activation", out=acc, in_=psflat[:, base + 1:base + 1 + TB], func=AF.Identity,
                               bias=cw[:, 3:4], scale=cw[:, 1:2]), rk, [ak])
                P.add("dve", I("scalar_tensor_tensor", out=acc, in0=psflat[:, base:base + TB], scalar=cw[:, 0:1],
                               in1=acc, op0=ALU.mult, op1=ALU.add), rk + [ak], [ak])
                P.add("dve", I("scalar_tensor_tensor", out=acc, in0=psflat[:, base + 2:base + 2 + TB],
                               scalar=cw[:, 2:3], in1=acc, op0=ALU.mult, op1=ALU.add), rk + [ak], [ak])
            P.add("act", I("activation", out=accs[0], in_=accs[0], func=AF.Silu),
                  [("cacc", k % 2, 0)], [("cacc", k % 2, 0)])
            P.add("dve", I("tensor_tensor", out=actT[:, j, :], in0=accs[0], in1=accs[1], op=ALU.mult),
                  [("cacc", k % 2, 0), ("cacc", k % 2, 1)], [("actT", j)])
        for nb in range(2):
            dma_pool(wdn.rearrange("p j n -> p (j n)"), w_dn[nb], wdn_sem, writes=["wdn"])
            for tl in range(TB // 128):
                tt = hf * (TB // 128) + tl
                b = 6 + (octr[0] % 2)
                for j in range(NFC):
                    P.add("pe", I("matmul", psf(b), lhsT=actT[:, j, tl * 128:(tl + 1) * 128], rhs=wdn[:, j, :],
                                  start=(j == 0), stop=(j == NFC - 1)),
                          ["wdn"] + [("actT", jj) for jj in range(NFC)], [pskey(b)])
                oi = octr[0] % 2; octr[0] += 1
                ot = otile[oi]
                P.add("dve", I("tensor_tensor", out=ot, in0=psf(b), in1=xmid[:, tt, nb * 512:(nb + 1) * 512],
                               op=ALU.add), [pskey(b)], [("ot", oi)])
                dma_sp(out[tt * 128:(tt + 1) * 128, nb * 512:(nb + 1) * 512], ot, osem[oi], reads=[("ot", oi)])


def _host_consts():
    ident = np.eye(128, dtype=np.float32)
    blk = np.zeros((128, 128), np.float32)
    blk[:64, :64] = 1.0 / 64
    blk[64:, 64:] = 1.0 / 64
    p = np.arange(128, dtype=np.float64)[:, None]
    x = np.arange(896, dtype=np.float64)[None, :]
    Tm = (-np.abs(x - p - 384)).astype(np.float32)
    bias = np.zeros((128, H, NBL + NBR), np.float64)
    bqt = np.zeros((128, H, 8), np.float64)
    for h in range(H):
        sl = SLOPES[h]
        for j in range(NBL):
            bias[:, h, j] = sl * (p[:, 0] - 128.0 * j)
        for j in range(NBR):
            bias[:, h, NBL + j] = -sl * (p[:, 0] + 128.0 * j + 1.0)
        for i in range(4):
            bqt[:, h, i] = np.exp(-sl * (128.0 * i + p[:, 0]))
            bqt[:, h, 4 + i] = np.exp(-sl * (511.0 - 128.0 * i - p[:, 0]))
    return (ident, blk, Tm, bias.reshape(128, -1).astype(np.float32),
            bqt.reshape(128, -1).astype(np.float32))


def _pk(w):
    n = w.shape[1]
    return np.ascontiguousarray(w.reshape(8, 128, n).transpose(1, 0, 2).reshape(128, 8 * n))


def prepare_inputs(x, ln_mix_g, w_in, q_norm_g, k_norm_g, lambda_q1, lambda_k1, lambda_q2, lambda_k2,
                   subln_g, sg_norm_g, sg_w, sg_b, w_branch_a, w_branch_b, w_out, ln_ffn_g, w_up, conv_w,
                   conv_b, w_down):
    f = lambda a: np.asarray(a, dtype=np.float32)
    x = f(x); w_in = f(w_in)[0]; w_up = f(w_up)[0]; w_down = f(w_down)[0]
    ident, blk, Tm, bias, bqt = _host_consts()
    w_qkv = np.stack([_pk(np.concatenate([w_in[:, h * 128:(h + 1) * 128],
                                          w_in[:, 1024 + h * 128:1024 + (h + 1) * 128],
                                          w_in[:, 2048 + h * 128:2048 + (h + 1) * 128]], axis=1)) for h in range(H)])
    w_us = np.stack([_pk(np.concatenate([w_in[:, 3072 + g * 128:3072 + (g + 1) * 128],
                                         w_in[:, 4096 + g * 128:4096 + (g + 1) * 128]], axis=1)) for g in range(8)])
    wa = f(w_branch_a)[0]; wb = f(w_branch_b)[0]
    w_ab = np.stack([_pk(np.concatenate([wa[:, c * 128:(c + 1) * 128], wb[:, c * 128:(c + 1) * 128],
                                         w_in[:, 5120 + c * 128:5120 + (c + 1) * 128],
                                         w_in[:, 6144 + c * 128:6144 + (c + 1) * 128]], axis=1)) for c in range(8)])
    w_o = _pk(f(w_out)[0])
    w_upp = np.stack([_pk(np.concatenate([w_up[:, j * 128:(j + 1) * 128],
                                          w_up[:, DFF + j * 128:DFF + (j + 1) * 128]], axis=1)) for j in range(NFC)])
    wd = w_down.reshape(NFC, 128, 1024).transpose(1, 0, 2)
    w_dn = np.stack([np.ascontiguousarray(wd[:, :, nb * 512:(nb + 1) * 512]).reshape(128, NFC * 512) for nb in range(2)])
    vec = np.zeros((128, 64), np.float32)
    vec[:, 0:8] = f(ln_mix_g)[0].reshape(8, 128).T
    vec[:, 8] = np.concatenate([f(q_norm_g)[0]] * 2)
    vec[:, 9] = np.concatenate([f(k_norm_g)[0]] * 2)
    vec[:, 10] = f(subln_g)[0]
    vec[:, 11:19] = f(sg_norm_g)[0].reshape(8, 128).T
    vec[:, 19:27] = f(ln_ffn_g)[0].reshape(8, 128).T
    vec[:, 27] = -0.5
    lam = np.concatenate([f(lambda_q1)[0], f(lambda_k1)[0], f(lambda_q2)[0], f(lambda_k2)[0]])
    lamr = np.ascontiguousarray(np.broadcast_to(lam[None, :], (128, 256)))
    sgw0 = f(sg_w)[0]; sgb0 = f(sg_b)[0]; cw0 = f(conv_w)[0]; cb0 = f(conv_b)[0]
    per_parity = []
    for par in range(2):
        sgw_, sgb_, cw_ = sgw0, sgb0, cw0
        if par == 1:
            sgw_ = sgw0[:, ::-1, ::-1]
            sgb_ = sgb0[:, ::-1]
            cw_ = cw0[::-1, :]
        sgwT = np.ascontiguousarray(sgw_.transpose(2, 0, 1)).reshape(128, 1024)
        sgbr = np.ascontiguousarray(np.broadcast_to(sgb_.reshape(1, 1024), (128, 1024)))
        conv = np.zeros((128, 2, NFC, 4), np.float32)
        for gv in range(2):
            for j in range(NFC):
                cols = slice(gv * DFF + j * 128, gv * DFF + (j + 1) * 128)
                conv[:, gv, j, 0:3] = cw_[:, cols].T
                conv[:, gv, j, 3] = cb0[cols]
        per_parity.append((sgwT, sgbr, conv.reshape(128, -1)))
    in_maps = []
    for c in range(8):
        b, par = c // 2, c % 2
        xs_ = x[b] if par == 0 else x[b, ::-1]
        sgwT, sgbr, conv = per_parity[par]
        in_maps.append({
            "xs": np.ascontiguousarray(xs_), "c_ident": ident, "c_blk": blk, "c_T": Tm, "c_bias": bias,
            "c_bq": bqt, "c_vec": vec, "c_lam": lamr, "c_sgb": sgbr, "c_conv": conv,
            "w_qkv": w_qkv, "w_us": w_us, "w_ab": w_ab, "w_out": w_o, "w_up": w_upp, "w_dn": w_dn,
            "sg_wT": sgwT,
        })
    return in_maps


_NC_CACHE = {}


def kernel(**inputs):
    in_maps = prepare_inputs(**inputs)
    if "nc" not in _NC_CACHE:
        _NC_CACHE["nc"] = build()
    nc = _NC_CACHE["nc"]
    res = run_bass_kernel_spmd(nc, in_maps, core_ids=list(range(8)))
    outp = np.zeros((4, S, D), np.float32)
    for c in range(8):
        b, par = c // 2, c % 2
        o = np.asarray(res.results[c]["out"], dtype=np.float32)
        if par == 0:
            outp[b, 0:2048] = o
        else:
            outp[b, 2048:] = o[::-1]
    return outp
```

```python
import math
import numpy as np
import concourse.bass as bass
import concourse.mybir as mybir
from concourse.bass_utils import run_bass_kernel_spmd

F32 = mybir.dt.float32
BF16 = mybir.dt.bfloat16
AF = mybir.ActivationFunctionType
ALU = mybir.AluOpType

D = 1024
S = 4096
H = 8
DFF = 2816
NFC = DFF // 128
EPS = 1e-6
LAM_INIT = 0.8 - 0.6 * math.exp(0.0)
NT = 32
NTO = 17
TOWN = NTO * 128
TOTH = S - TOWN
VW = 130
SLOPES = [2.0 ** (-(h + 1)) for h in range(H)]
NBL = 18
NBR = 32
USE_V_FILLERS = False
SKIP_T = 128.0


class _Sem:
    def __init__(self, handle):
        self.handle = handle
        self.count = 0


class _Op:
    __slots__ = ("eng", "fn", "deps", "sem", "val", "signal", "is_dma", "idx")

    def __init__(self, eng, fn, is_dma):
        self.eng = eng
        self.fn = fn
        self.deps = []
        self.sem = None
        self.val = 0
        self.signal = False
        self.is_dma = is_dma


class Prog:
    ENGS = ("pe", "act", "dve", "pool", "sp")

    def __init__(self, esems):
        self.esems = esems
        self.ops = {e: [] for e in self.ENGS}
        self.last_w = {}
        self.readers = {}
        self.pending = {e: [] for e in self.ENGS}
        self.dsems = []

    def dsem(self, handle, group=False):
        s = _Sem(handle)
        s.group = group
        s.gops = []
        self.dsems.append(s)
        return s

    def add(self, eng, fn, reads=(), writes=(), dsem=None):
        op = _Op(eng, fn, dsem is not None)
        deps = []
        for k in reads:
            w = self.last_w.get(k)
            if w is not None:
                deps.append(w)
        for k in writes:
            w = self.last_w.get(k)
            if w is not None:
                deps.append(w)
            deps.extend(self.readers.get(k, ()))
        deps.extend(self.pending[eng])
        self.pending[eng] = []
        latest = {}
        seen = set()
        for d in deps:
            if d is op or id(d) in seen:
                continue
            seen.add(id(d))
            if d.eng == eng and eng == "pe" and not d.is_dma:
                continue
            if d.is_dma:
                op.deps.append(d)
            else:
                cur = latest.get(d.eng)
                if cur is None or d.idx > cur.idx:
                    latest[d.eng] = d
        op.deps.extend(latest.values())
        for k in reads:
            self.readers.setdefault(k, []).append(op)
        for k in writes:
            self.last_w[k] = op
            self.readers[k] = []
        if dsem is not None:
            dsem.count += 1
            op.sem = dsem
            op.val = 16 * dsem.count
            op.signal = True
            dsem.last = op
            dsem.gops.append(op)
        else:
            op.sem = self.esems[eng]
        op.idx = len(self.ops[eng])
        self.ops[eng].append(op)
        return op

    def barrier(self):
        lasts = []
        for e in self.ENGS:
            for op in reversed(self.ops[e]):
                if not op.is_dma:
                    lasts.append(op)
                    break
        for s in self.dsems:
            if getattr(s, "last", None) is not None:
                lasts.append(s.last)
        for e in self.ENGS:
            self.pending[e] = list(lasts)

    def emit(self, block, final_waits):
        for s in self.dsems:
            if s.group:
                for op in s.gops:
                    op.val = 16 * s.count
        for e in self.ENGS:
            for op in self.ops[e]:
                for d in op.deps:
                    d.signal = True
        for e in self.ENGS:
            c = 0
            for op in self.ops[e]:
                if not op.is_dma and op.signal:
                    c += 1
                    op.val = c

        def run(name, eng):
            waited = {}
            for op in self.ops[name]:
                need = {}
                for d in op.deps:
                    if need.get(d.sem, 0) < d.val:
                        need[d.sem] = d.val
                for s, v in need.items():
                    if waited.get(s, 0) < v:
                        eng.wait_ge(s.handle, v)
                        waited[s] = v
                ins = op.fn(eng)
                if op.is_dma:
                    ins.then_inc(op.sem.handle, 16)
                elif op.signal:
                    ins.then_inc(op.sem.handle, 1)
            if name == "sp":
                for s in final_waits:
                    eng.wait_ge(s.handle, 16 * s.count)

        @block.tensor
        def _(t):
            run("pe", t)

        @block.scalar
        def _(a):
            run("act", a)

        @block.vector
        def _(v):
            run("dve", v)

        @block.gpsimd
        def _(g):
            run("pool", g)

        @block.sync
        def _(s):
            run("sp", s)


class Arena:
    def __init__(self, t, nwords):
        self.t = t
        self.n = nwords

    def f32(self, off, n):
        assert off + n <= self.n, (off, n, self.n)
        return self.t[:, off:off + n]

    def bf16(self, off, n):
        w = (n + 1) // 2
        assert off + w <= self.n, (off, w, self.n)
        return self.t[:, off:off + w].bitcast(BF16)


def I(method, *args, **kwargs):
    def fn(e):
        return getattr(e, method)(*args, **kwargs)
    return fn


def _blocks(total, width=512):
    out = []
    s = 0
    while s < total:
        w = min(width, total - s)
        out.append((s, w))
        s += w
    return out


def build(debug=None, stop_after=None):
    debug = debug or ()
    nc = bass.Bass("TRN2", target_bir_lowering=False)

    def din(name, shape):
        return nc.dram_tensor(name, list(shape), F32, kind="ExternalInput").ap()

    xs = din("xs", [S, D])
    c_ident = din("c_ident", [128, 128])
    c_blk = din("c_blk", [128, 128])
    c_T = din("c_T", [128, 896])
    c_bias = din("c_bias", [128, H * (NBL + NBR)])
    c_bq = din("c_bq", [128, H * 8])
    c_vec = din("c_vec", [128, 64])
    c_lam = din("c_lam", [128, 256])
    c_sgb = din("c_sgb", [128, 1024])
    c_conv = din("c_conv", [128, 2 * NFC * 4])
    w_qkv = din("w_qkv", [H, 128, 8 * 384])
    w_us = din("w_us", [8, 128, 8 * 256])
    w_ab = din("w_ab", [8, 128, 8 * 512])
    w_out = din("w_out", [128, 8 * 1024])
    w_up = din("w_up", [NFC, 128, 8 * 256])
    w_dn = din("w_dn", [2, 128, NFC * 512])
    sg_wT = din("sg_wT", [128, 8 * 128])
    out = nc.dram_tensor("out", [2048, D], F32, kind="ExternalOutput").ap()
    dbg = {}
    dbg_shapes = {"hT": [128, 8 * TOWN], "onT": [128, 8 * TOWN], "obT": [128, 8 * TOWN],
                  "mT": [128, 8 * TOWN], "xmid": [128, 16 * 1024], "KT": [128, S],
                  "QT": [128, TOWN], "V": [128, NT * VW], "h2T": [128, 8 * TOWN]}
    for k in debug:
        dbg[k] = nc.dram_tensor("dbg_" + k, dbg_shapes[k], F32, kind="ExternalOutput").ap()

    NW = 53200
    import contextlib
    with contextlib.ExitStack() as es:
        arena_t = es.enter_context(nc.sbuf_tensor("arena", [128, NW], F32))
        ps_all = es.enter_context(nc.psum_tensor("ps_all", [128, 8, 512], F32))
        sem_names = ["pe", "act", "dve", "pool"]
        esems = {n: _Sem(es.enter_context(nc.semaphore("s_" + n))) for n in sem_names}
        esems["sp"] = _Sem(es.enter_context(nc.semaphore("s_sp")))
        P = Prog(esems)

        def new_dsem(name, group=False):
            return P.dsem(es.enter_context(nc.semaphore("d_" + name)), group=group)

        ar = Arena(arena_t, NW)

        o = 0
        def take(nwords):
            nonlocal o
            r = o
            o += nwords
            return r
        ident = ar.bf16(take(64), 128)
        blk = ar.bf16(take(64), 128)
        Tm = ar.f32(take(896), 896)
        biasT = ar.f32(take(H * (NBL + NBR)), H * (NBL + NBR))
        bq = ar.f32(take(H * 8), H * 8)
        vec = ar.f32(take(64), 64)
        lamv = ar.f32(take(256), 256)
        sgb = ar.f32(take(1024), 1024)
        convp = ar.f32(take(2 * NFC * 4), 2 * NFC * 4)
        sgw = ar.bf16(take(512), 1024)
        small = ar.f32(take(256), 256)
        CONST_END = o
        R0 = CONST_END
        KB = 256
        hT_own_off = R0
        onT_off = R0 + 34 * KB
        hT_oth_off = R0 + 68 * KB
        obT_off = hT_oth_off
        TMP = R0 + 102 * KB
        hT_own = ar.bf16(hT_own_off, 8 * TOWN).rearrange("p (c t) -> p c t", c=8)
        hT_oth = ar.bf16(hT_oth_off, 8 * TOTH).rearrange("p (c t) -> p c t", c=8)
        onT = ar.bf16(onT_off, 8 * TOWN).rearrange("p (c t) -> p c t", c=8)
        obT = ar.bf16(obT_off, 8 * TOWN).rearrange("p (c t) -> p c t", c=8)

        V_GMIX, V_GQ, V_GK, V_SUB, V_SGN, V_GFFN, V_NEGH = 0, 8, 9, 10, 11, 19, 27
        gmix = vec[:, V_GMIX:V_GMIX + 8]
        gffn = vec[:, V_GFFN:V_GFFN + 8]
        sgn = vec[:, V_SGN:V_SGN + 8]
        negh = vec[:, V_NEGH:V_NEGH + 1]
        gqs = small[:, 0:1]
        subs = small[:, 1:2]
        neglam = small[:, 2:3]
        d12 = small[:, 3:5]
        e12 = small[:, 5:7]
        ljunk = small[:, 8:72]

        dconst = new_dsem("const", group=True)
        dconst2 = new_dsem("const2", group=True)
        out_sems = []

        def dma_sp(out_ap, in_ap, sem, reads=(), writes=()):
            return P.add("sp", I("dma_start", out=out_ap, in_=in_ap), reads, writes, dsem=sem)

        def dma_pool(out_ap, in_ap, sem, reads=(), writes=()):
            return P.add("pool", I("dma_start", out=out_ap, in_=in_ap), reads, writes, dsem=sem)

        dma_pool(ident, c_ident, dconst2, writes=["ident"])
        dma_pool(blk, c_blk, dconst2, writes=["blk"])
        dma_pool(sgw, sg_wT, dconst2, writes=["sgw"])
        dma_sp(Tm, c_T, dconst, writes=["Tm"])
        dma_sp(biasT, c_bias, dconst, writes=["biasT"])
        dma_sp(bq, c_bq, dconst, writes=["bq"])
        dma_sp(vec, c_vec, dconst, writes=["vec"])
        dma_sp(lamv, c_lam, dconst, writes=["lamv"])
        dma_sp(sgb, c_sgb, dconst, writes=["sgb"])
        dma_sp(convp, c_conv, dconst, writes=["convp"])

        P.add("dve", I("tensor_scalar", out=gqs, in0=vec[:, V_GQ:V_GQ + 1], scalar1=0.125,
                                               scalar2=None, op0=ALU.mult), ["vec"], ["gqs"])
        P.add("dve", I("tensor_scalar", out=subs, in0=vec[:, V_SUB:V_SUB + 1],
                                               scalar1=float(1.0 - LAM_INIT), scalar2=None, op0=ALU.mult),
              ["vec"], ["subs"])
        P.add("dve", I("scalar_tensor_tensor", out=ljunk, in0=lamv[:, 0:64], scalar=1.0,
                                                      in1=lamv[:, 64:128], op0=ALU.mult, op1=ALU.mult,
                                                      accum_out=d12[:, 0:1]), ["lamv"], ["d1", "ljunk"])
        P.add("dve", I("scalar_tensor_tensor", out=ljunk, in0=lamv[:, 128:192], scalar=1.0,
                                                      in1=lamv[:, 192:256], op0=ALU.mult, op1=ALU.mult,
                                                      accum_out=d12[:, 1:2]), ["lamv"], ["d2", "ljunk"])
        P.add("act", I("activation", out=e12, in_=d12, func=AF.Exp), ["d1", "d2"], ["e12"])
        P.add("dve", I("tensor_tensor", out=neglam, in0=e12[:, 1:2], in1=e12[:, 0:1], op=ALU.subtract),
              ["e12"], ["nl0"])
        P.add("dve", I("tensor_scalar", out=neglam, in0=neglam, scalar1=float(-LAM_INIT), scalar2=None,
                                               op0=ALU.add), ["nl0"], ["neglam"])

        def psf(b):
            return ps_all[:, b, :]

        def psb(b):
            return ps_all[:, b, :].bitcast(BF16)

        bank_ctr = [0]

        def next_bank(lo=0, hi=8):
            b = lo + bank_ctr[0] % (hi - lo)
            bank_ctr[0] += 1
            return b

        def pskey(b):
            return ("ps", b)

        t0 = TMP
        NXT = 4
        xt_bufs = [ar.f32(t0 + i * 1024, 1024) for i in range(NXT)]
        t0 += NXT * 1024
        xn_bufs = [ar.bf16(t0 + i * 512, 1024) for i in range(2)]
        sqj = ar.bf16(t0 + 1024, 1024)
        stat = ar.f32(t0 + 1536, 128)
        PH_A = t0 + 1536 + 128
        xt_sems = [new_dsem("xt%d" % i) for i in range(NXT)]
        nctr = [0]

        def norm_tile(src_key, src_ap, gvec, dstT, col0, dst_key):
            i = nctr[0]
            nctr[0] += 1
            sc = (i % 32) * 4
            ss, ms, rs = stat[:, sc:sc + 1], stat[:, sc + 1:sc + 2], stat[:, sc + 2:sc + 3]
            xn = xn_bufs[i % 2]
            xnk = ("xn", i % 2)
            hold = {}

            def st_a():
                P.add("act", I("activation", out=sqj, in_=src_ap, func=AF.Square, accum_out=ss),
                      [src_key], [("ss", sc), "sqj"])

            def st_b():
                P.add("dve", I("tensor_scalar", out=ms, in0=ss, scalar1=1.0 / D, scalar2=EPS,
                               op0=ALU.mult, op1=ALU.add), [("ss", sc)], [("ms", sc)])
                P.add("pool", I("tensor_tensor", out=rs, in0=ms, in1=negh, op=ALU.pow),
                      [("ms", sc), "vec"], [("rs", sc)])

            def st_c():
                P.add("dve", I("tensor_scalar", out=xn, in0=src_ap, scalar1=rs, scalar2=None, op0=ALU.mult),
                      [src_key, ("rs", sc)], [xnk])
                b = next_bank()
                hold["b"] = b
                pv = psb(b)
                for c in range(8):
                    P.add("pe", I("transpose", out=pv[:, c * 128:(c + 1) * 128], in_=xn[:, c * 128:(c + 1) * 128],
                                  identity=ident), [xnk, "ident"], [pskey(b)])

            def st_d():
                pv = psb(hold["b"])
                P.add("dve", I("tensor_tensor", out=dstT[:, :, col0:col0 + 128],
                               in0=pv.rearrange("p (c t) -> p c t", c=8),
                               in1=gvec.unsqueeze(2).to_broadcast([128, 8, 128]), op=ALU.mult),
                      [pskey(hold["b"]), "vec"], [dst_key])

            return (st_a, st_b, st_c, st_d)

        if "onT" in debug:
            P.add("pool", I("memset", ar.bf16(onT_off, 8 * TOWN), 0.0), [], [("onT", hh, qq) for hh in range(H) for qq in range(NTO)])
        p0 = []
        for tt in range(NT):
            bi = tt % NXT
            xt = xt_bufs[bi]
            if tt < NTO:
                st = norm_tile(("xt", bi), xt, gmix, hT_own, tt * 128, ("hT", tt))
            else:
                st = norm_tile(("xt", bi), xt, gmix, hT_oth, (tt - NTO) * 128, ("hT", tt))
            p0.append((tt, bi, xt, st))
        for n in range(NT + 3):
            for si in range(4):
                k_ = n - si
                if 0 <= k_ < NT:
                    tt, bi, xt, st = p0[k_]
                    if si == 0:
                        dma_sp(xt, xs[tt * 128:(tt + 1) * 128, :], xt_sems[bi], writes=[("xt", bi)])
                    st[si]()

        def hT_block(tok0, w):
            if tok0 < TOWN:
                assert tok0 + w <= TOWN
                return hT_own, tok0, [("hT", t) for t in range(tok0 // 128, (tok0 + w + 127) // 128)]
            return hT_oth, tok0 - TOWN, [("hT", t) for t in range(tok0 // 128, (tok0 + w + 127) // 128)]

        own_blocks = _blocks(TOWN)
        all_blocks = own_blocks + [(TOWN + s, w) for s, w in _blocks(TOTH)]

        def dump(name, src_flat):
            P.barrier()
            s_ = new_dsem("dbg_" + name); out_sems.append(s_)
            dma_pool(dbg[name], src_flat, s_)
            P.barrier()

        if "hT" in debug:
            dump("hT", ar.bf16(hT_own_off, 8 * TOWN))

        a0 = PH_A
        KT = ar.bf16(a0, S); a0 += S // 2
        Va_flats = [ar.bf16(a0 + i * ((NT * VW) // 2), NT * VW) for i in range(2)]
        Va_bufs = [f_.rearrange("p (t v) -> p t v", v=VW) for f_ in Va_flats]
        Va_flat = Va_flats[0]
        a0 += NT * VW
        QT = ar.bf16(a0, TOWN); a0 += TOWN // 2
        wq_bufs = [ar.bf16(a0 + i * 1536, 3072).rearrange("p (c n) -> p c n", c=8) for i in range(2)]
        a0 += 3072
        wq_sems = [new_dsem("wq%d" % i) for i in range(2)]
        sq_bufs = [ar.bf16(a0 + i * 256, 512) for i in range(2)]; a0 += 512
        sd_bufs = [ar.f32(a0 + i * 512, 512) for i in range(2)]; a0 += 1024
        rs_bufs = [ar.f32(a0 + i * 512, 512) for i in range(2)]; a0 += 1024
        al = TMP
        NPB = 3
        Pb = [ar.bf16(al + i * 512, 1024).rearrange("p (m n) -> p m n", m=2) for i in range(NPB)]; al += NPB * 512
        sDb = [ar.f32(al + i * 1024, 1024).rearrange("p (m n) -> p m n", m=2) for i in range(2)]; al += 2048
        tots = [ar.f32(al + i * 8 * VW, 8 * VW).rearrange("p (i m v) -> p i m v", i=4, m=2) for i in range(2)]
        al += 16 * VW
        assert al <= PH_A
        osb = [ar.f32(a0 + i * 128, 128) for i in range(2)]; a0 += 256
        ojunk = ar.f32(a0, 128); a0 += 128
        o2b = [ar.f32(a0 + i * 128, 128) for i in range(2)]; a0 += 256
        onb = [ar.bf16(a0 + i * 64, 128) for i in range(4)]; a0 += 256
        est = ar.f32(a0, 64); a0 += 64
        Mh = ar.f32(a0, 896); a0 += 896
        assert a0 <= NW, a0

        for Va_ in Va_bufs:
            P.add("pool", I("memset", Va_[:, :, 128:129], 1.0), [], ["Vones"])
            P.add("pool", I("memset", Va_[:, :, 129:130], 0.0), [], ["Vpad"])

        pctr = [0]

        def qk_project(h, wq, wkey, col_lo, blocks, gvecap, gkey, dst, dst_key_fn):
            steps = []
            for (s0, w) in blocks:
                steps.append(_qk_block(wq, wkey, col_lo, s0, w, gvecap, gkey, dst, dst_key_fn))
            for n in range(len(steps) + 1):
                if n < len(steps):
                    steps[n][0]()
                if n >= 1:
                    steps[n - 1][1]()

        def _qk_block(wq, wkey, col_lo, s0, w, gvecap, gkey, dst, dst_key_fn):
            src, so, hkeys = hT_block(s0, w)
            i = pctr[0]; pctr[0] += 1
            bA = pbank[0] % 4; bB = 4 + (pbank[0] % 4); pbank[0] += 1
            pA = psf(bA)[:, 0:w]; pB = psf(bB)[:, 0:w]
            sq = sq_bufs[i % 2][:, 0:w]; sd = sd_bufs[i % 2][:, 0:w]

            def stage1():
                for c in range(8):
                    P.add("pe", I("matmul", pA, lhsT=wq[:, c, col_lo:col_lo + 128], rhs=src[:, c, so:so + w],
                                  start=(c == 0), stop=(c == 7)), hkeys + [wkey], [pskey(bA)])
                P.add("act", I("activation", out=sq, in_=pA, func=AF.Square), [pskey(bA)], [("sq", i % 2)])

            def stage2():
                P.add("pe", I("matmul", pB, lhsT=blk, rhs=sq, start=True, stop=True), [("sq", i % 2), "blk"], [pskey(bB)])
                rs = rs_bufs[i % 2][:, 0:w]
                P.add("act", I("activation", out=sd, in_=pB, func=AF.Ln, bias=EPS_AP, scale=1.0),
                      [pskey(bB), "epsap"], [("sd", i % 2)])
                P.add("act", I("activation", out=rs, in_=sd, func=AF.Exp, scale=-0.5),
                      [("sd", i % 2)], [("rsb", i % 2)])
                P.add("dve", I("scalar_tensor_tensor", out=dst[:, s0:s0 + w], in0=pA, scalar=gvecap, in1=rs,
                               op0=ALU.mult, op1=ALU.mult), [pskey(bA), ("rsb", i % 2), gkey], [dst_key_fn(s0)])

            return (stage1, stage2)

        pbank = [0]

        EPS_AP = small[:, 7:8]
        P.add("pool", I("memset", EPS_AP, EPS), [], ["epsap"])

        gctr = [0]
        dctr = [0]
        ectr = [0]
        tpctr = [0]
        psflat = ps_all.rearrange("p b n -> p (b n)")

        def acc_off(a):
            return (4 + a // 3) * 512 + (a % 3) * VW

        def acc_tile(a):
            o_ = acc_off(a)
            return psflat[:, o_:o_ + VW]

        def acc_pair(i):
            o0 = acc_off(2 * i)
            st = acc_off(2 * i + 1) - o0
            return psflat[:, o0:o0 + 2 * st].rearrange("p (m s) -> p m s", m=2)[:, :, 0:VW]

        def acc_bank(a):
            return 4 + a // 3

        blkctr = [0]
        pending_combine = []

        def make_jobs(h, q0, W):
            slope = SLOPES[h]
            nq = W // 128
            blk_id = blkctr[0]
            tb = blkctr[0] % 2; blkctr[0] += 1

            def far(kt):
                k_lo, k_hi = kt * 128, kt * 128 + 127
                dist = (q0 - k_hi) if k_hi < q0 else (k_lo - (q0 + W - 1))
                return slope * dist >= SKIP_T
            Lt = [kt for kt in range(NT) if (kt + 1) * 128 <= q0 and not far(kt)]
            Dt = [kt for kt in range(NT) if q0 <= kt * 128 < q0 + W]
            Rt = [kt for kt in range(NT) if kt * 128 >= q0 + W and not far(kt)]
            phases = [(ph, tl) for ph, tl in (("L", Lt), ("D", Dt), ("R", Rt)) if tl]
            jobs = []
            for pi, (phase, tiles) in enumerate(phases):
                for n, kt in enumerate(tiles):
                    buf = gctr[0] % 2; pbuf = gctr[0] % NPB; gctr[0] += 1
                    jobs.append(_make_job(h, q0, W, nq, slope, phase, pi == 0, pi == len(phases) - 1,
                                          n, len(tiles), kt, buf, tb, pbuf) + (blk_id, pi == 0 and n == 0))
            return jobs

        def _make_job(h, q0, W, nq, slope, phase, first_phase, last_phase, n, ntiles, kt, buf, tb, pbuf):
            b0 = 2 * buf
            Skey = ("S", buf)
            Pk = ("P", pbuf)
            tot = tots[tb]

            def S_fn():
                P.add("pe", I("matmul", ps_all[:, b0, 0:W], lhsT=KT[0:64, kt * 128:(kt + 1) * 128],
                              rhs=QT[0:64, q0:q0 + W], start=True, stop=True, tile_position=(0, 0)),
                      ["KT", "QT"], [pskey(b0), Skey])
                P.add("pe", I("matmul", ps_all[:, b0 + 1, 0:W], lhsT=KT[64:128, kt * 128:(kt + 1) * 128],
                              rhs=QT[64:128, q0:q0 + W], start=True, stop=True, tile_position=(64, 0)),
                      ["KT", "QT"], [pskey(b0 + 1), Skey])

            def combine():
                for i in range(nq):
                    accv = acc_pair(i)
                    tv = tot[:, i, :, :]
                    tk = ("tot", tb, i)
                    if phase == "L":
                        sc = bq[:, h * 8 + i: h * 8 + i + 1]
                    elif phase == "R":
                        ii = i if W == 512 else 3
                        sc = bq[:, h * 8 + 4 + ii: h * 8 + 4 + ii + 1]
                    else:
                        sc = None
                    rk = [pskey(acc_bank(2 * i)), pskey(acc_bank(2 * i + 1)), ("acc", 2 * i), ("acc", 2 * i + 1), "bq"]
                    if first_phase:
                        if sc is None:
                            P.add("dve", I("tensor_copy", out=tv, in_=accv), rk, [tk])
                        else:
                            P.add("dve", I("tensor_scalar", out=tv, in0=accv, scalar1=sc, scalar2=None,
                                           op0=ALU.mult), rk, [tk])
                    else:
                        if sc is None:
                            P.add("dve", I("tensor_tensor", out=tv, in0=accv, in1=tv, op=ALU.add), rk + [tk], [tk])
                        else:
                            P.add("dve", I("scalar_tensor_tensor", out=tv, in0=accv, scalar=sc, in1=tv,
                                           op0=ALU.mult, op1=ALU.add), rk + [tk], [tk])

            def act_fn():
                Sin = ps_all[:, b0:b0 + 2, 0:W]
                pout = Pb[pbuf][:, :, 0:W]
                if phase == "D":
                    jD = kt - q0 // 128
                    db = dctr[0] % 2; dctr[0] += 1
                    sd_ = sDb[db][:, :, 0:W]
                    msl = Mh[:, 384 - 128 * jD: 384 - 128 * jD + W]
                    P.add("act", I("activation", out=sd_, in_=Sin, func=AF.Exp),
                          [Skey, pskey(b0), pskey(b0 + 1)], [("sD", db)])
                    P.add("dve", I("tensor_tensor", out=pout, in0=sd_,
                                   in1=msl.unsqueeze(1).to_broadcast([128, 2, W]), op=ALU.mult),
                          [("sD", db), "Mh"], [(Pk, 0), (Pk, 1)])
                else:
                    if phase == "L":
                        j = (q0 - kt * 128) // 128
                        col = h * (NBL + NBR) + j
                    else:
                        j = (kt * 128 - q0 - W) // 128
                        col = h * (NBL + NBR) + NBL + j
                    bap = biasT[:, col:col + 1]
                    P.add("act", I("activation", out=pout, in_=Sin, func=AF.Exp, bias=bap, scale=1.0),
                          [Skey, pskey(b0), pskey(b0 + 1), "biasT"], [(Pk, 0), (Pk, 1)])
            def av_fn():
                while pending_combine:
                    pending_combine.pop(0)()
                for i in range(nq):
                    for m in range(2):
                        a = 2 * i + m
                        P.add("pe", I("matmul", acc_tile(a), lhsT=Pb[pbuf][:, m, i * 128:(i + 1) * 128],
                                      rhs=Va_bufs[h % 2][:, kt, :], start=(n == 0 and a % 3 == 0),
                                      stop=(n == ntiles - 1), skip_group_check=True),
                              [(Pk, m), ("V", h % 2), "Vones", "Vpad"], [pskey(acc_bank(a)), ("acc", a)])
                if n == ntiles - 1:
                    pending_combine.append(combine)

            deferred = []
            if n == ntiles - 1 and last_phase:
                for i in range(nq):
                    k = ectr[0]; ectr[0] += 1
                    qt = q0 // 128 + i
                    deferred.append((2 + i, _epi_a(tot, tb, i, k), "a"))
                    deferred.append((4 + i, _epi_b(k), "b"))
                    deferred.append((6 + i, _epi_tail(h, k, qt), "t"))
            is_last_D = (phase == "D" and n == ntiles - 1)
            return (S_fn, act_fn, av_fn, deferred, is_last_D)

        def _epi_slots(k):
            eb = (k % 8) * 8
            return (est[:, eb:eb + 2], est[:, eb + 2:eb + 3], est[:, eb + 3:eb + 4], est[:, eb + 4:eb + 5],
                    est[:, eb + 5:eb + 6], ("est", k % 8))

        def _epi_a(tot, tb, i, k):
            rz, nl2, ss, ms, rs, ek = _epi_slots(k)
            o1 = osb[0]; o2 = o2b[k % 2]
            tk = ("tot", tb, i)

            def fn():
                P.add("dve", I("reciprocal", out=rz, in_=tot[:, i, :, 128]), [tk], [ek])
                P.add("dve", I("tensor_scalar", out=nl2, in0=rz[:, 1:2], scalar1=neglam, scalar2=None,
                               op0=ALU.mult), [ek, "neglam"], [(ek, "nl2")])
                P.add("dve", I("tensor_scalar", out=o1, in0=tot[:, i, 0, 0:128], scalar1=rz[:, 0:1],
                               scalar2=None, op0=ALU.mult), [tk, ek], ["o1"])
                P.add("dve", I("scalar_tensor_tensor", out=o2, in0=tot[:, i, 1, 0:128], scalar=nl2, in1=o1,
                               op0=ALU.mult, op1=ALU.add), [tk, (ek, "nl2"), "o1"], [("o2", k % 2)])
                P.add("dve", I("scalar_tensor_tensor", out=ojunk, in0=o2, scalar=1.0, in1=o2, op0=ALU.mult,
                               op1=ALU.mult, accum_out=ss), [("o2", k % 2)], [(ek, "ss"), "ojunk"])
                P.add("dve", I("tensor_scalar", out=ms, in0=ss, scalar1=1.0 / 128, scalar2=EPS, op0=ALU.mult,
                               op1=ALU.add), [(ek, "ss")], [(ek, "ms")])
                P.add("pool", I("tensor_tensor", out=rs, in0=ms, in1=negh, op=ALU.pow),
                      [(ek, "ms"), "vec"], [(ek, "rs")])
            return fn

        def _epi_b(k):
            rz, nl2, ss, ms, rs, ek = _epi_slots(k)

            def fn():
                P.add("dve", I("tensor_scalar", out=onb[k % 4], in0=o2b[k % 2], scalar1=rs, scalar2=None,
                               op0=ALU.mult), [("o2", k % 2), (ek, "rs")], [("on", k % 4)])
            return fn

        def _epi_tail(h, k, qt):
            def fn():
                sl = tpctr[0] % 8; tpctr[0] += 1
                pv = psb(7)[:, sl * 128:(sl + 1) * 128]
                P.add("pe", I("transpose", out=pv, in_=onb[k % 4], identity=ident),
                      [("on", k % 4), "ident"], [pskey(7)])
                P.add("dve", I("tensor_scalar", out=onT[:, h, qt * 128:(qt + 1) * 128], in0=pv, scalar1=subs,
                               scalar2=None, op0=ALU.mult), [pskey(7), "subs"], [("onT", h, qt)])
            return fn

        def run_pipeline(jobs, fillers=(), fill_pos=()):
            pend = []
            fillers = list(fillers)
            nfill = [0]
            N_ = len(jobs)
            for n in range(N_ + 2):
                if n >= 2:
                    jobs[n - 2][1]()
                if n < N_:
                    jobs[n][0]()
                if n >= 2:
                    if jobs[n - 2][6]:
                        keep = []
                        for it in pend:
                            if it[4] <= jobs[n - 2][5] - 2:
                                it[1]()
                            else:
                                keep.append(it)
                        pend = keep
                    jobs[n - 2][2]()
                    if jobs[n - 2][4]:
                        for it in pend:
                            it[2] = True
                    for (d, fn, kind) in jobs[n - 2][3]:
                        pend.append([d, fn, False, kind, jobs[n - 2][5]])
                    if (n - 2) in fill_pos and fillers:
                        fillers.pop(0)()
                        nfill[0] += 1
                group_open = (nfill[0] % 4) != 0
                keep = []
                for it in pend:
                    if it[2]:
                        it[0] -= 1
                    if it[2] and it[0] <= 0 and not (it[3] == "t" and group_open):
                        it[1]()
                    else:
                        keep.append(it)
                pend = keep
            while pending_combine:
                pending_combine.pop(0)()
            while (nfill[0] % 4) != 0 and fillers:
                fillers.pop(0)()
                nfill[0] += 1
            for it in pend:
                it[1]()
            for f_ in fillers:
                f_()

        def v_group(h, g4, bank, eng):
            wq = wq_bufs[h % 2]
            wkey = ("wq", h % 2)
            Va_ = Va_bufs[h % 2]

            def part(j):
                def fn():
                    tt = g4 * 4 + j
                    src, so, hkeys = hT_block(tt * 128, 128)
                    for c in range(8):
                        P.add("pe", I("matmul", psf(bank)[:, j * 128:(j + 1) * 128], lhsT=src[:, c, so:so + 128],
                                      rhs=wq[:, c, 256:384], start=(c == 0), stop=(c == 7), skip_group_check=True),
                              hkeys + [wkey], [pskey(bank)])
                    if j != 3:
                        return
                    src_v = psf(bank).rearrange("p (j d) -> p j d", j=4)
                    dst_v = Va_[:, g4 * 4:(g4 + 1) * 4, 0:128]
                    if eng == "act":
                        P.add("act", I("activation", out=dst_v, in_=src_v, func=AF.Copy), [pskey(bank)], [("V", h % 2)])
                    else:
                        P.add("dve", I("tensor_copy", out=dst_v, in_=src_v), [pskey(bank)], [("V", h % 2)])
                return fn
            return [part(j) for j in range(4)]

        def load_wq(h):
            dma_pool(wq_bufs[h % 2].rearrange("p c n -> p (c n)"), w_qkv[h], wq_sems[h % 2], writes=[("wq", h % 2)])

        load_wq(0)
        for h in range(H):
            wq = wq_bufs[h % 2]
            wkey = ("wq", h % 2)
            qk_project(h, wq, wkey, 128, all_blocks, vec[:, V_GK:V_GK + 1], "vec", KT, lambda s0: "KT")
            qk_project(h, wq, wkey, 0, own_blocks, gqs, "gqs", QT, lambda s0: "QT")
            if h == 0 or not USE_V_FILLERS:
                for g4 in range(NT // 4):
                    for f_ in v_group(h, g4, next_bank(), "act"):
                        f_()
            if h + 1 < H:
                load_wq(h + 1)
            if h == 0:
                for nm, src_ap in (("KT", KT), ("QT", QT), ("V", Va_flat)):
                    if nm in debug:
                        dump(nm, src_ap)
            if stop_after == "proj0":
                break
            P.add("act", I("activation", out=Mh, in_=Tm, func=AF.Exp, scale=float(SLOPES[h])), ["Tm"], ["Mh"])
            jobs = []
            fill_pos = []
            for (q0, W) in own_blocks:
                jb = make_jobs(h, q0, W)
                base = len(jobs)
                if len(jb) >= 22:
                    fill_pos += [base + 10 + i_ for i_ in range(4)] + [base + 16 + i_ for i_ in range(4)]
                elif len(jb) >= 15:
                    fill_pos += [base + 10 + i_ for i_ in range(4)]
                jobs.extend(jb)
            fillers = []
            if USE_V_FILLERS and h + 1 < H and stop_after != "head0":
                fillers = [f_ for g4 in range(NT // 4) for f_ in v_group(h + 1, g4, 7, "dve")]
            run_pipeline(jobs, fillers, set(fill_pos))
            if stop_after == "head0":
                break

        if "onT" in debug:
            dump("onT", ar.bf16(onT_off, 8 * TOWN))

        done = stop_after in ("proj0", "head0", "attn")
        if not done:
            build_rest(nc, P, ar, ps_all, locals())

        with nc.Block() as block:
            P.emit(block, out_sems)
    return nc


def build_rest(nc, P, ar, ps_all, L):
    names = ("TMP hT_own onT obT sgw sgb sgn vec negh ident gffn convp xs w_us w_ab w_out w_up w_dn out debug dbg "
             "new_dsem dma_sp dma_pool psf psb pskey next_bank own_blocks small R0 KB out_sems dump obT_off").split()
    (TMP, hT_own, onT, obT, sgw, sgb, sgn, vec, negh, ident, gffn, convp, xs, w_us, w_ab, w_out, w_up, w_dn, out,
     debug, dbg, new_dsem, dma_sp, dma_pool, psf, psb, pskey, next_bank, own_blocks, small, R0, KB, out_sems,
     dump, obT_off) = [L[n] for n in names]
    NW = ar.n
    stop_after = L["stop_after"]

    P.barrier()
    b0 = TMP
    mT = ar.bf16(b0, 8 * TOWN).rearrange("p (c t) -> p c t", c=8); b0 += 4 * TOWN
    uT_bufs = [ar.f32(b0 + i * TOWN, TOWN) for i in range(2)]; b0 += 2 * TOWN
    wus_bufs = [ar.bf16(b0 + i * 1024, 2048).rearrange("p (c n) -> p c n", c=8) for i in range(2)]; b0 += 2048
    wus_sems = [new_dsem("wus%d" % i) for i in range(2)]
    gl_bufs = [ar.f32(b0 + i * 128, 128) for i in range(4)]; b0 += 512
    gjunk = ar.f32(b0, 128); b0 += 128
    svh_bufs = [ar.bf16(b0 + i * 64, 128) for i in range(4)]; b0 += 256
    mtmp = ar.f32(b0, 512); b0 += 512
    gst = ar.f32(b0, 64); b0 += 64
    B2_OFF = b0
    wab_bufs = [ar.bf16(B2_OFF + i * 2048, 4096).rearrange("p (c n) -> p c n", c=8) for i in range(2)]
    wab_sems = [new_dsem("wab%d" % i) for i in range(2)]

    def _load_wab(fc):
        dma_pool(wab_bufs[fc % 2].rearrange("p c n -> p (c n)"), w_ab[fc], wab_sems[fc % 2], writes=[("wab", fc % 2)])

    _load_wab(0)
    sctr = [0]
    mixctr = [0]
    svbank = [0]

    def _sgu_tile(g, wus, wkey, t4, tiles, j, tt, bm):
        uT = uT_bufs[g % 2]
        k = sctr[0]; sctr[0] += 1
        b = svbank[0] % 6; svbank[0] += 1
        pv = psf(b)[:, 0:128]
        gl = gl_bufs[k % 4]; svh = svh_bufs[k % 4]
        sb_ = (k % 8) * 4
        ss, ms, rs = gst[:, sb_:sb_ + 1], gst[:, sb_ + 1:sb_ + 2], gst[:, sb_ + 2:sb_ + 3]
        gk = ("gst", k % 8)

        def proj():
            for c in range(8):
                P.add("pe", I("matmul", pv, lhsT=hT_own[:, c, tt * 128:(tt + 1) * 128], rhs=wus[:, c, 128:256],
                              start=(c == 0), stop=(c == 7)), [("hT", tt), wkey], [pskey(b)])
            P.add("act", I("activation", out=gl, in_=pv, func=AF.Gelu), [pskey(b)], [("gl", k % 4)])
            P.add("dve", I("scalar_tensor_tensor", out=gjunk, in0=gl, scalar=1.0, in1=gl, op0=ALU.mult,
                           op1=ALU.mult, accum_out=ss), [("gl", k % 4)], [(gk, "ss"), "gjunk"])
            P.add("dve", I("tensor_scalar", out=ms, in0=ss, scalar1=1.0 / 128, scalar2=EPS, op0=ALU.mult,
                           op1=ALU.add), [(gk, "ss")], [(gk, "ms")])
            P.add("pool", I("tensor_tensor", out=rs, in0=ms, in1=negh, op=ALU.pow), [(gk, "ms"), "vec"], [(gk, "rs")])

        def proj_b():
            P.add("dve", I("tensor_scalar", out=svh, in0=gl, scalar1=rs, scalar2=None, op0=ALU.mult),
                  [("gl", k % 4), (gk, "rs")], [("svh", k % 4)])

        def mix():
            P.add("pe", I("matmul", psf(bm)[:, j * 128:(j + 1) * 128], lhsT=svh, rhs=sgw[:, g * 128:(g + 1) * 128],
                          start=True, stop=True, skip_group_check=True), [("svh", k % 4), "sgw"], [pskey(bm)])
            if j != len(tiles) - 1:
                return
            n = len(tiles)
            c0 = t4 * 128
            mt = mtmp[:, 0:n * 128]
            P.add("dve", I("scalar_tensor_tensor", out=mt.rearrange("p (j t) -> p j t", j=n),
                           in0=psf(bm)[:, 0:n * 128].rearrange("p (j t) -> p j t", j=n), scalar=sgn[:, g:g + 1],
                           in1=sgb[:, g * 128:(g + 1) * 128].unsqueeze(1).to_broadcast([128, n, 128]),
                           op0=ALU.mult, op1=ALU.add), [pskey(bm), "vec", "sgb"], ["mtmp"])
            P.add("dve", I("tensor_tensor", out=obT[:, g, c0:c0 + n * 128], in0=mt, in1=uT[:, c0:c0 + n * 128],
                           op=ALU.mult), ["mtmp"] + [("uT", g % 2, s_) for s_, _ in own_blocks], [("obT", g, t4)])

        return (proj, proj_b, mix)

    def _load_wus(g):
        dma_pool(wus_bufs[g % 2].rearrange("p c n -> p (c n)"), w_us[g], wus_sems[g % 2], writes=[("wus", g % 2)])

    _load_wus(0)

    def _u_proj(g, wus, wkey):
        def fn():
            if g + 1 < 8:
                _load_wus(g + 1)
            for (s0, w) in own_blocks:
                b = svbank[0] % 6; svbank[0] += 1
                pv = psf(b)[:, 0:w]
                hk = [("hT", t) for t in range(s0 // 128, (s0 + w) // 128)]
                for c in range(8):
                    P.add("pe", I("matmul", pv, lhsT=wus[:, c, 0:128], rhs=hT_own[:, c, s0:s0 + w],
                                  start=(c == 0), stop=(c == 7)), hk + [wkey], [pskey(b)])
                P.add("act", I("activation", out=uT_bufs[g % 2][:, s0:s0 + w], in_=pv, func=AF.Gelu),
                      [pskey(b)], [("uT", g % 2, s0)])
        return fn

    steps = []
    for g in range(8):
        wus = wus_bufs[g % 2]
        wkey = ("wus", g % 2)
        first = True
        for t4 in range(0, NTO, 4):
            tiles = list(range(t4, min(t4 + 4, NTO)))
            bm = 6 + (mixctr[0] % 2); mixctr[0] += 1
            for j, tt in enumerate(tiles):
                st = _sgu_tile(g, wus, wkey, t4, tiles, j, tt, bm)
                steps.append((st[0], st[1], st[2], _u_proj(g, wus, wkey) if first else None))
                first = False
    for n in range(len(steps) + 3):
        if n < len(steps):
            if steps[n][3] is not None:
                steps[n][3]()
            steps[n][0]()
        if 1 <= n <= len(steps):
            steps[n - 1][1]()
        if 3 <= n:
            steps[n - 3][2]()

    if "obT" in debug:
        dump("obT", ar.bf16(obT_off, 8 * TOWN))
    P.barrier()

    wout_pre = ar.bf16(TMP + 4 * TOWN, 8192)
    wout_sem = new_dsem("wout")
    dma_pool(wout_pre, w_out, wout_sem, writes=["wout"])
    b0 = B2_OFF + 4096
    sg_bufs = [ar.f32(b0 + i * 512, 512) for i in range(2)]; b0 += 1024
    t12 = [ar.f32(b0 + i * 512, 512) for i in range(2)]; b0 += 1024
    assert b0 <= NW, b0
    allh = [("hT", t) for t in range(NTO)]
    for fc in range(8):
        wab = wab_bufs[fc % 2]
        wkey = ("wab", fc % 2)
        if fc + 1 < 8:
            _load_wab(fc + 1)
        for (s0, w) in own_blocks:
            banks = [next_bank() for _ in range(4)]
            srcs = [onT, obT, hT_own, hT_own]
            for q in range(4):
                pv = psf(banks[q])[:, 0:w]
                for c in range(8):
                    P.add("pe", I("matmul",
                        pv, lhsT=wab[:, c, q * 128:(q + 1) * 128], rhs=srcs[q][:, c, s0:s0 + w],
                        start=(c == 0), stop=(c == 7)),
                        [wkey, ("srcB2", q)], [pskey(banks[q])])
            for q in (2, 3):
                P.add("act", I("activation",
                    out=sg_bufs[q - 2][:, 0:w], in_=psf(banks[q])[:, 0:w], func=AF.Sigmoid),
                    [pskey(banks[q])], [("sg", q)])
            for q in (0, 1):
                P.add("dve", I("tensor_tensor",
                    out=t12[q][:, 0:w], in0=psf(banks[q])[:, 0:w], in1=sg_bufs[q][:, 0:w], op=ALU.mult),
                    [pskey(banks[q]), ("sg", q + 2)], [("t12", q)])
            P.add("pool", I("tensor_tensor",
                out=mT[:, fc, s0:s0 + w], in0=t12[0][:, 0:w], in1=t12[1][:, 0:w], op=ALU.add),
                [("t12", 0), ("t12", 1)], [("mT", fc, s0)])

    if "mT" in debug:
        dump("mT", ar.bf16(TMP, 8 * TOWN))

    P.barrier()
    xmid = ar.f32(R0, 16 * 1024).rearrange("p (t f) -> p t f", t=16)
    h2T = ar.bf16(R0 + 64 * KB, 8 * TOWN).rearrange("p (c t) -> p c t", c=8)
    c0 = TMP + 4 * TOWN
    wout = ar.bf16(c0, 8192).rearrange("p (c n) -> p c n", c=8); c0 += 4096
    xt2 = [ar.f32(c0 + i * 1024, 1024) for i in range(2)]; c0 += 2048
    xhalo = ar.f32(c0, 1024); c0 += 1024
    xn2 = [ar.bf16(c0 + i * 512, 1024) for i in range(2)]; c0 += 1024
    sqj2 = ar.bf16(c0, 1024); c0 += 512
    st2 = ar.f32(c0, 128); c0 += 128
    C_OFF = c0
    assert c0 <= NW
    xt2_sems = [new_dsem("xt2_%d" % i) for i in range(2)]
    def _b3_tile(tt):
        bi = tt % 2
        dst = xmid[:, tt, :] if tt < 16 else xhalo
        dkey = ("xmid", tt)
        sc = (tt % 32) * 4
        ss, ms, rs = st2[:, sc:sc + 1], st2[:, sc + 1:sc + 2], st2[:, sc + 2:sc + 3]
        xn = xn2[tt % 2]
        hold = {}

        def st_a():
            dma_sp(xt2[bi], xs[tt * 128:(tt + 1) * 128, :], xt2_sems[bi], writes=[("xt2", bi)])
            for nb in range(2):
                b = next_bank()
                for c in range(8):
                    P.add("pe", I("matmul", psf(b), lhsT=mT[:, c, tt * 128:(tt + 1) * 128],
                                  rhs=wout[:, c, nb * 512:(nb + 1) * 512], start=(c == 0), stop=(c == 7)),
                          ["wout", ("mTall",)], [pskey(b)])
                P.add("dve", I("tensor_tensor", out=dst[:, nb * 512:(nb + 1) * 512], in0=psf(b),
                               in1=xt2[bi][:, nb * 512:(nb + 1) * 512], op=ALU.add),
                      [pskey(b), ("xt2", bi)], [(dkey, nb)])
            P.add("act", I("activation", out=sqj2, in_=dst, func=AF.Square, accum_out=ss),
                  [(dkey, 0), (dkey, 1)], [("ss2", sc), "sqj2"])

        def st_b():
            P.add("dve", I("tensor_scalar", out=ms, in0=ss, scalar1=1.0 / D, scalar2=EPS, op0=ALU.mult,
                           op1=ALU.add), [("ss2", sc)], [("ms2", sc)])
            P.add("pool", I("tensor_tensor", out=rs, in0=ms, in1=negh, op=ALU.pow), [("ms2", sc), "vec"], [("rs2", sc)])

        def st_c():
            P.add("dve", I("tensor_scalar", out=xn, in0=dst, scalar1=rs, scalar2=None, op0=ALU.mult),
                  [(dkey, 0), (dkey, 1), ("rs2", sc)], [("xn2", tt % 2)])
            b = next_bank()
            hold["b"] = b
            pv = psb(b)
            for c in range(8):
                P.add("pe", I("transpose", out=pv[:, c * 128:(c + 1) * 128], in_=xn[:, c * 128:(c + 1) * 128],
                              identity=ident), [("xn2", tt % 2), "ident"], [pskey(b)])

        def st_d():
            b = hold["b"]
            pv = psb(b)
            P.add("dve", I("tensor_tensor", out=h2T[:, :, tt * 128:(tt + 1) * 128],
                           in0=pv.rearrange("p (c t) -> p c t", c=8),
                           in1=gffn.unsqueeze(2).to_broadcast([128, 8, 128]), op=ALU.mult),
                  [pskey(b), "vec"], [("h2T", tt)])

        return (st_a, st_b, st_c, st_d)

    b3 = [_b3_tile(tt) for tt in range(NTO)]
    for n in range(NTO + 3):
        for si in range(4):
            k_ = n - si
            if 0 <= k_ < NTO:
                b3[k_][si]()

    if "xmid" in debug or "h2T" in debug:
        P.barrier()
        if "xmid" in debug:
            s_ = new_dsem("dbg_xmid"); out_sems.append(s_)
            dma_sp(dbg["xmid"], xmid.rearrange("p t f -> p (t f)"), s_, reads=[])
        if "h2T" in debug:
            dump("h2T", ar.bf16(R0 + 64 * KB, 8 * TOWN))
        P.barrier()

    P.barrier()
    c0 = R0 + 98 * KB
    TB = 1024
    actT = ar.bf16(c0, NFC * TB).rearrange("p (j t) -> p j t", j=NFC); c0 += NFC * TB // 2
    wdn = ar.bf16(c0, NFC * 512).rearrange("p (j n) -> p j n", j=NFC); c0 += NFC * 256
    wup_bufs = [ar.bf16(c0 + i * 1024, 2048).rearrange("p (c n) -> p c n", c=8) for i in range(2)]; c0 += 2048
    cacc = [[ar.f32(c0 + (pp * 2 + i) * TB, TB) for i in range(2)] for pp in range(2)]; c0 += 4 * TB
    otile = [ar.f32(c0 + i * 512, 512) for i in range(2)]; c0 += 1024
    assert c0 <= NW, c0
    wup_sems = [new_dsem("wup%d" % i) for i in range(2)]
    wdn_sem = new_dsem("wdn")
    osem = [new_dsem("out%d" % i) for i in range(2)]
    out_sems.extend(osem)
    allh2 = [("h2T", t) for t in range(NTO)]
    uctr = [0]
    octr = [0]
    psflat = ps_all.rearrange("p b n -> p (b n)")
    for hf in range(2):
        a = hf * TB
        if hf == 0:
            segs = [(1, 0, 511), (512, 511, 512), (1024, 1023, 2)]
            for gv in range(2):
                P.add("dve", I("memset", psflat[:, gv * 1536:gv * 1536 + 1], 0.0), [], [pskey(3 * gv)])
        else:
            segs = [(0, a - 1, 512), (512, a + 511, 512), (1024, a + 1023, 2)]
        for j in range(NFC):
            k = uctr[0]; uctr[0] += 1
            wup = wup_bufs[k % 2]
            wkey = ("wup", k % 2)
            dma_pool(wup.rearrange("p c n -> p (c n)"), w_up[j], wup_sems[k % 2], writes=[wkey])
            accs = cacc[k % 2]
            for gv in range(2):
                base = gv * 1536
                banks = [3 * gv, 3 * gv + 1, 3 * gv + 2]
                for (dc, t0_, n) in segs:
                    b = 3 * gv + dc // 512
                    col = base + dc
                    for c in range(8):
                        P.add("pe", I("matmul", psflat[:, col:col + n], lhsT=wup[:, c, gv * 128:(gv + 1) * 128],
                                      rhs=h2T[:, c, t0_:t0_ + n], start=(c == 0), stop=(c == 7)),
                              [wkey] + allh2, [pskey(b)])
                cw = convp[:, (gv * NFC + j) * 4:(gv * NFC + j) * 4 + 4]
                acc = accs[gv]
                ak = ("cacc", k % 2, gv)
                rk = [pskey(b) for b in banks] + ["convp"]
                P.add("act", I("activation", out=acc, in_=psflat[:, base + 1:base + 1 + TB], func=AF.Identity,
                               bias=cw[:, 3:4], scale=cw[:, 1:2]), rk, [ak])
                P.add("dve", I("scalar_tensor_tensor", out=acc, in0=psflat[:, base:base + TB], scalar=cw[:, 0:1],
                               in1=acc, op0=ALU.mult, op1=ALU.add), rk + [ak], [ak])
                P.add("dve", I("scalar_tensor_tensor", out=acc, in0=psflat[:, base + 2:base + 2 + TB],
                               scalar=cw[:, 2:3], in1=acc, op0=ALU.mult, op1=ALU.add), rk + [ak], [ak])
            P.add("act", I("activation", out=accs[0], in_=accs[0], func=AF.Silu),
                  [("cacc", k % 2, 0)], [("cacc", k % 2, 0)])
            P.add("dve", I("tensor_tensor", out=actT[:, j, :], in0=accs[0], in1=accs[1], op=ALU.mult),
                  [("cacc", k % 2, 0), ("cacc", k % 2, 1)], [("actT", j)])
        for nb in range(2):
            dma_pool(wdn.rearrange("p j n -> p (j n)"), w_dn[nb], wdn_sem, writes=["wdn"])
            for tl in range(TB // 128):
                tt = hf * (TB // 128) + tl
                b = 6 + (octr[0] % 2)
                for j in range(NFC):
                    P.add("pe", I("matmul", psf(b), lhsT=actT[:, j, tl * 128:(tl + 1) * 128], rhs=wdn[:, j, :],
                                  start=(j == 0), stop=(j == NFC - 1)),
                          ["wdn"] + [("actT", jj) for jj in range(NFC)], [pskey(b)])
                oi = octr[0] % 2; octr[0] += 1
                ot = otile[oi]
                P.add("dve", I("tensor_tensor", out=ot, in0=psf(b), in1=xmid[:, tt, nb * 512:(nb + 1) * 512],
                               op=ALU.add), [pskey(b)], [("ot", oi)])
                dma_sp(out[tt * 128:(tt + 1) * 128, nb * 512:(nb + 1) * 512], ot, osem[oi], reads=[("ot", oi)])


def _host_consts():
    ident = np.eye(128, dtype=np.float32)
    blk = np.zeros((128, 128), np.float32)
    blk[:64, :64] = 1.0 / 64
    blk[64:, 64:] = 1.0 / 64
    p = np.arange(128, dtype=np.float64)[:, None]
    x = np.arange(896, dtype=np.float64)[None, :]
    Tm = (-np.abs(x - p - 384)).astype(np.float32)
    bias = np.zeros((128, H, NBL + NBR), np.float64)
    bqt = np.zeros((128, H, 8), np.float64)
    for h in range(H):
        sl = SLOPES[h]
        for j in range(NBL):
            bias[:, h, j] = sl * (p[:, 0] - 128.0 * j)
        for j in range(NBR):
            bias[:, h, NBL + j] = -sl * (p[:, 0] + 128.0 * j + 1.0)
        for i in range(4):
            bqt[:, h, i] = np.exp(-sl * (128.0 * i + p[:, 0]))
            bqt[:, h, 4 + i] = np.exp(-sl * (511.0 - 128.0 * i - p[:, 0]))
    return (ident, blk, Tm, bias.reshape(128, -1).astype(np.float32),
            bqt.reshape(128, -1).astype(np.float32))


def _pk(w):
    n = w.shape[1]
    return np.ascontiguousarray(w.reshape(8, 128, n).transpose(1, 0, 2).reshape(128, 8 * n))


def prepare_inputs(x, ln_mix_g, w_in, q_norm_g, k_norm_g, lambda_q1, lambda_k1, lambda_q2, lambda_k2,
                   subln_g, sg_norm_g, sg_w, sg_b, w_branch_a, w_branch_b, w_out, ln_ffn_g, w_up, conv_w,
                   conv_b, w_down):
    f = lambda a: np.asarray(a, dtype=np.float32)
    x = f(x); w_in = f(w_in)[0]; w_up = f(w_up)[0]; w_down = f(w_down)[0]
    ident, blk, Tm, bias, bqt = _host_consts()
    w_qkv = np.stack([_pk(np.concatenate([w_in[:, h * 128:(h + 1) * 128],
                                          w_in[:, 1024 + h * 128:1024 + (h + 1) * 128],
                                          w_in[:, 2048 + h * 128:2048 + (h + 1) * 128]], axis=1)) for h in range(H)])
    w_us = np.stack([_pk(np.concatenate([w_in[:, 3072 + g * 128:3072 + (g + 1) * 128],
                                         w_in[:, 4096 + g * 128:4096 + (g + 1) * 128]], axis=1)) for g in range(8)])
    wa = f(w_branch_a)[0]; wb = f(w_branch_b)[0]
    w_ab = np.stack([_pk(np.concatenate([wa[:, c * 128:(c + 1) * 128], wb[:, c * 128:(c + 1) * 128],
                                         w_in[:, 5120 + c * 128:5120 + (c + 1) * 128],
                                         w_in[:, 6144 + c * 128:6144 + (c + 1) * 128]], axis=1)) for c in range(8)])
    w_o = _pk(f(w_out)[0])
    w_upp = np.stack([_pk(np.concatenate([w_up[:, j * 128:(j + 1) * 128],
                                          w_up[:, DFF + j * 128:DFF + (j + 1) * 128]], axis=1)) for j in range(NFC)])
    wd = w_down.reshape(NFC, 128, 1024).transpose(1, 0, 2)
    w_dn = np.stack([np.ascontiguousarray(wd[:, :, nb * 512:(nb + 1) * 512]).reshape(128, NFC * 512) for nb in range(2)])
    vec = np.zeros((128, 64), np.float32)
    vec[:, 0:8] = f(ln_mix_g)[0].reshape(8, 128).T
    vec[:, 8] = np.concatenate([f(q_norm_g)[0]] * 2)
    vec[:, 9] = np.concatenate([f(k_norm_g)[0]] * 2)
    vec[:, 10] = f(subln_g)[0]
    vec[:, 11:19] = f(sg_norm_g)[0].reshape(8, 128).T
    vec[:, 19:27] = f(ln_ffn_g)[0].reshape(8, 128).T
    vec[:, 27] = -0.5
    lam = np.concatenate([f(lambda_q1)[0], f(lambda_k1)[0], f(lambda_q2)[0], f(lambda_k2)[0]])
    lamr = np.ascontiguousarray(np.broadcast_to(lam[None, :], (128, 256)))
    sgw0 = f(sg_w)[0]; sgb0 = f(sg_b)[0]; cw0 = f(conv_w)[0]; cb0 = f(conv_b)[0]
    per_parity = []
    for par in range(2):
        sgw_, sgb_, cw_ = sgw0, sgb0, cw0
        if par == 1:
            sgw_ = sgw0[:, ::-1, ::-1]
            sgb_ = sgb0[:, ::-1]
            cw_ = cw0[::-1, :]
        sgwT = np.ascontiguousarray(sgw_.transpose(2, 0, 1)).reshape(128, 1024)
        sgbr = np.ascontiguousarray(np.broadcast_to(sgb_.reshape(1, 1024), (128, 1024)))
        conv = np.zeros((128, 2, NFC, 4), np.float32)
        for gv in range(2):
            for j in range(NFC):
                cols = slice(gv * DFF + j * 128, gv * DFF + (j + 1) * 128)
                conv[:, gv, j, 0:3] = cw_[:, cols].T
                conv[:, gv, j, 3] = cb0[cols]
        per_parity.append((sgwT, sgbr, conv.reshape(128, -1)))
    in_maps = []
    for c in range(8):
        b, par = c // 2, c % 2
        xs_ = x[b] if par == 0 else x[b, ::-1]
        sgwT, sgbr, conv = per_parity[par]
        in_maps.append({
            "xs": np.ascontiguousarray(xs_), "c_ident": ident, "c_blk": blk, "c_T": Tm, "c_bias": bias,
            "c_bq": bqt, "c_vec": vec, "c_lam": lamr, "c_sgb": sgbr, "c_conv": conv,
            "w_qkv": w_qkv, "w_us": w_us, "w_ab": w_ab, "w_out": w_o, "w_up": w_upp, "w_dn": w_dn,
            "sg_wT": sgwT,
        })
    return in_maps


_NC_CACHE = {}


def kernel(**inputs):
    in_maps = prepare_inputs(**inputs)
    if "nc" not in _NC_CACHE:
        _NC_CACHE["nc"] = build()
    nc = _NC_CACHE["nc"]
    res = run_bass_kernel_spmd(nc, in_maps, core_ids=list(range(8)))
    outp = np.zeros((4, S, D), np.float32)
    for c in range(8):
        b, par = c // 2, c % 2
        o = np.asarray(res.results[c]["out"], dtype=np.float32)
        if par == 0:
            outp[b, 0:2048] = o
        else:
            outp[b, 2048:] = o[::-1]
    return outp
```
